# Optimizing a Trainium2 kernel written in Bass

```python
import math
import jax, jax.numpy as jnp
from jax import lax
import numpy as np

D_MODEL = 1024
BATCH = 4
SEQ = 8192
DEPTH = 1

MLA_HEADS = 8
MLA_Q_LORA = 256
MLA_KV_LORA = 128
MLA_NOPE = 64
MLA_ROPE = 32
MLA_V = 64
SWA_HEADS = 8
SWA_KV_HEADS = 2
SWA_HEAD_DIM = 64
WINDOW = 128
BLOCK = 128
REL_BUCKETS = 32
REL_MAX_DIST = 128
ROPE_THETA = 10000.0
EPS = 1e-6

MLA_WIDTH = MLA_HEADS * MLA_V
SWA_WIDTH = SWA_HEADS * SWA_HEAD_DIM
MIX_WIDTH = MLA_WIDTH + SWA_WIDTH
IN_SPLITS = (MLA_Q_LORA, MLA_KV_LORA, MLA_ROPE,
             SWA_HEADS * SWA_HEAD_DIM, SWA_KV_HEADS * SWA_HEAD_DIM, SWA_KV_HEADS * SWA_HEAD_DIM,
             MIX_WIDTH)
IN_WIDTH = sum(IN_SPLITS)

kernel_name = "hymba_mla_swa_sink_gated"


def _offsets(sizes):
    out, acc = [], 0
    for s in sizes[:-1]:
        acc += s
        out.append(acc)
    return tuple(out)


def rmsnorm(x, g):
    xf = x.astype(jnp.float32)
    y = xf * lax.rsqrt(jnp.mean(xf * xf, axis=-1, keepdims=True) + EPS)
    return (y * g.astype(jnp.float32)).astype(x.dtype)


def rope(x, pos):
    half = x.shape[-1] // 2
    inv = ROPE_THETA ** (-jnp.arange(half, dtype=jnp.float32) / half)
    ang = pos.astype(jnp.float32)[..., None] * inv
    ang = ang.reshape(ang.shape[:2] + (1,) * (x.ndim - 3) + (half,))
    cos, sin = jnp.cos(ang), jnp.sin(ang)
    xf = x.astype(jnp.float32)
    x1, x2 = xf[..., :half], xf[..., half:]
    return jnp.concatenate([x1 * cos - x2 * sin, x2 * cos + x1 * sin], axis=-1).astype(x.dtype)


def t5_bucket(dist):
    n = jnp.maximum(dist, 0)
    max_exact = REL_BUCKETS // 2
    nf = jnp.maximum(n, 1).astype(jnp.float32)
    large = max_exact + (jnp.log(nf / max_exact) / math.log(REL_MAX_DIST / max_exact)
                         * (REL_BUCKETS - max_exact)).astype(jnp.int32)
    large = jnp.minimum(large, REL_BUCKETS - 1)
    return jnp.where(n < max_exact, n, large)


def mla_group(q_lat, kv_lat, k_rope, positions, q_a_norm, w_q_b, kv_a_norm, w_kv_b):
    B, S, _ = q_lat.shape
    q = (rmsnorm(q_lat, q_a_norm) @ w_q_b).reshape(B, S, MLA_HEADS, MLA_NOPE + MLA_ROPE)
    q_nope, q_pe = q[..., :MLA_NOPE], rope(q[..., MLA_NOPE:], positions)
    kv = (rmsnorm(kv_lat, kv_a_norm) @ w_kv_b).reshape(B, S, MLA_HEADS, MLA_NOPE + MLA_V)
    k_nope, v = kv[..., :MLA_NOPE], kv[..., MLA_NOPE:]
    k_pe = rope(k_rope, positions)
    scale = (MLA_NOPE + MLA_ROPE) ** -0.5
    nb = S // BLOCK
    qn_blocks = jnp.moveaxis(q_nope.reshape(B, nb, BLOCK, MLA_HEADS, MLA_NOPE), 1, 0)
    qp_blocks = jnp.moveaxis(q_pe.reshape(B, nb, BLOCK, MLA_HEADS, MLA_ROPE), 1, 0)
    key_idx = jnp.arange(S)

    def block_attn(args):
        i, qn, qp = args
        s = (jnp.einsum('bqhd,bkhd->bhqk', qn, k_nope)
             + jnp.einsum('bqhr,bkr->bhqk', qp, k_pe)).astype(jnp.float32) * scale
        q_idx = i * BLOCK + jnp.arange(BLOCK)
        mask = key_idx[None, :] <= q_idx[:, None]
        s = jnp.where(mask[None, None], s, -jnp.inf)
        p = jax.nn.softmax(s, axis=-1).astype(v.dtype)
        return jnp.einsum('bhqk,bkhd->bqhd', p, v)

    o = lax.map(block_attn, (jnp.arange(nb), qn_blocks, qp_blocks))
    return jnp.moveaxis(o, 0, 1).reshape(B, S, MLA_WIDTH)


def _band(t, nb):
    pad = [(0, 0), (BLOCK, 0)] + [(0, 0)] * (t.ndim - 2)
    tp = jnp.pad(t, pad).reshape((t.shape[0], nb + 1, BLOCK) + t.shape[2:])
    return jnp.concatenate([tp[:, :-1], tp[:, 1:]], axis=2)


def swa_group(q, k, v, positions, rel_bias, sinks):
    B, S, _ = q.shape
    G = SWA_HEADS // SWA_KV_HEADS
    nb = S // BLOCK
    qb = q.reshape(B, nb, BLOCK, SWA_KV_HEADS, G, SWA_HEAD_DIM)
    kb = _band(k.reshape(B, S, SWA_KV_HEADS, SWA_HEAD_DIM), nb)
    vb = _band(v.reshape(B, S, SWA_KV_HEADS, SWA_HEAD_DIM), nb)
    kpos = _band(positions, nb)
    qpos = positions.reshape(B, nb, BLOCK)
    qi = jnp.arange(BLOCK)[:, None] + BLOCK
    ki = jnp.arange(2 * BLOCK)[None, :]
    delta = qi - ki
    blk = jnp.arange(nb)[:, None, None]
    valid = (delta >= 0) & (delta < WINDOW) & (blk * BLOCK + ki[None] - BLOCK >= 0)
    bucket = t5_bucket(qpos[..., :, None] - kpos[..., None, :])
    bias = jnp.take(rel_bias, bucket, axis=0)
    bias = jnp.moveaxis(bias, -1, 2).reshape(B, nb, SWA_KV_HEADS, G, BLOCK, 2 * BLOCK)
    s = jnp.einsum('bnqhgd,bnkhd->bnhgqk', qb, kb).astype(jnp.float32) * (SWA_HEAD_DIM ** -0.5)
    s = s + bias.astype(jnp.float32)
    s = jnp.where(valid[None, :, None, None], s, -jnp.inf)
    sink = sinks.astype(jnp.float32).reshape(1, 1, SWA_KV_HEADS, G, 1, 1)
    m = jnp.maximum(jnp.max(s, axis=-1, keepdims=True), sink)
    e = jnp.exp(s - m)
    p = e / (jnp.sum(e, axis=-1, keepdims=True) + jnp.exp(sink - m))
    o = jnp.einsum('bnhgqk,bnkhd->bnqhgd', p.astype(vb.dtype), vb)
    return o.reshape(B, S, SWA_WIDTH)


def setup_inputs(seed: int = 0) -> dict:
    key = jax.random.key(seed)
    ks = jax.random.split(key, 16)
    f32 = jnp.float32
    x = jax.random.normal(ks[0], (BATCH, SEQ, D_MODEL), f32)
    offset = jax.random.randint(ks[1], (BATCH, 1), 0, 1024, dtype=jnp.int32)
    positions = (offset + jnp.arange(SEQ, dtype=jnp.int32)[None, :]).astype(jnp.int32)
    norm_gain = 1.0 + 0.05 * jax.random.normal(ks[2], (DEPTH, D_MODEL), f32)
    w_in = jax.random.normal(ks[3], (DEPTH, D_MODEL, IN_WIDTH), f32) * D_MODEL ** -0.5
    q_a_norm = 1.0 + 0.05 * jax.random.normal(ks[4], (DEPTH, MLA_Q_LORA), f32)
    w_q_b = jax.random.normal(ks[5], (DEPTH, MLA_Q_LORA, MLA_HEADS * (MLA_NOPE + MLA_ROPE)), f32) * MLA_Q_LORA ** -0.5
    kv_a_norm = 1.0 + 0.05 * jax.random.normal(ks[6], (DEPTH, MLA_KV_LORA), f32)
    w_kv_b = jax.random.normal(ks[7], (DEPTH, MLA_KV_LORA, MLA_HEADS * (MLA_NOPE + MLA_V)), f32) * MLA_KV_LORA ** -0.5
    sinks = jax.random.normal(ks[8], (DEPTH, SWA_HEADS), f32)
    rel_bias = 0.5 * jax.random.normal(ks[9], (REL_BUCKETS, SWA_HEADS), f32)
    w_out = jax.random.normal(ks[10], (DEPTH, MIX_WIDTH, D_MODEL), f32) * MIX_WIDTH ** -0.5
    final_norm = 1.0 + 0.05 * jax.random.normal(ks[11], (D_MODEL,), f32)
    return {"x": x, "positions": positions, "norm_gain": norm_gain, "w_in": w_in,
            "q_a_norm": q_a_norm, "w_q_b": w_q_b, "kv_a_norm": kv_a_norm, "w_kv_b": w_kv_b,
            "sinks": sinks, "rel_bias": rel_bias, "w_out": w_out, "final_norm": final_norm}


def reference(x, positions, norm_gain, w_in, q_a_norm, w_q_b, kv_a_norm, w_kv_b,
              sinks, rel_bias, w_out, final_norm):
    offs = _offsets(IN_SPLITS)
    for layer in range(DEPTH):
        h = rmsnorm(x, norm_gain[layer])
        proj = h @ w_in[layer]
        q_lat, kv_lat, k_rope, q_s, k_s, v_s, gate = jnp.split(proj, offs, axis=-1)
        o_a = mla_group(q_lat, kv_lat, k_rope, positions, q_a_norm[layer], w_q_b[layer],
                        kv_a_norm[layer], w_kv_b[layer])
        o_b = swa_group(q_s, k_s, v_s, positions, rel_bias, sinks[layer])
        y = jnp.concatenate([o_a, o_b], axis=-1) * jax.nn.silu(gate)
        x = x + y @ w_out[layer]
    return rmsnorm(x, final_norm)
```

```python
import math
import types
from contextlib import ExitStack

import numpy as np
import concourse.bass as bass
import concourse.mybir as mybir
from concourse.bass_utils import run_bass_kernel_spmd

F32 = mybir.dt.float32
BF16 = mybir.dt.bfloat16
I32 = mybir.dt.int32
AF = mybir.ActivationFunctionType
ALU = mybir.AluOpType

D_MODEL = 1024
NKC = 8
IN_WIDTH = 2208
EPS = 1e-6
NEG = -30000.0
SBUF_BASE = 17408
SBUF_END = 229376
ENGS = ("pe", "act", "dve", "pool", "sp")
TWO_PI = 2.0 * math.pi
C1 = 6.28125
C2 = TWO_PI - C1
SKIP = set()
STOP_AFTER = None


class Buf:
    __slots__ = ("name", "w", "r", "sem", "semval")

    def __init__(self, name):
        self.name = name
        self.w = None
        self.r = []
        self.sem = None
        self.semval = 0


def _freeze(fn):
    if fn is None or fn.__closure__ is None:
        return fn
    cells = []
    for c in fn.__closure__:
        try:
            cells.append(types.CellType(c.cell_contents))
        except ValueError:
            cells.append(c)
    return types.FunctionType(fn.__code__, fn.__globals__, fn.__name__, fn.__defaults__, tuple(cells))


class Prog:
    def __init__(self):
        self.ops = {e: [] for e in ENGS}
        self.cnt = {e: 0 for e in ENGS}
        self.seen = {e: {} for e in ENGS}
        self.dma_bufs = []
        self.n_dma_sems = 0

    def _waits(self, eng, deps):
        need = {}
        for (k, v) in deps:
            if k == eng and eng == "pe":
                continue
            if v > need.get(k, 0):
                need[k] = v
        out = []
        seen = self.seen[eng]
        for k, v in need.items():
            if seen.get(k, 0) < v:
                seen[k] = v
                out.append((k, v))
        return out

    @staticmethod
    def _deps(reads, writes):
        deps = []
        for b in reads:
            if b.w is not None:
                deps.append(b.w)
        for b in writes:
            if b.w is not None:
                deps.append(b.w)
            deps.extend(b.r)
        return deps

    @staticmethod
    def _upd(tok, reads, writes):
        for b in reads:
            b.r.append(tok)
            if len(b.r) > 64:
                best = {}
                for (k, v) in b.r:
                    if v > best.get(k, 0):
                        best[k] = v
                b.r = list(best.items())
        for b in writes:
            b.w = tok
            b.r = []

    def op(self, eng, fn, reads=(), writes=()):
        waits = self._waits(eng, self._deps(reads, writes))
        self.cnt[eng] += 1
        tok = (eng, self.cnt[eng])
        self.ops[eng].append((_freeze(fn), waits, (eng, 1)))
        self._upd(tok, reads, writes)

    def dma(self, fn, reads=(), writes=(), q="sp"):
        waits = self._waits(q, self._deps(reads, writes))
        carrier = (list(writes) + list(reads))[0]
        if carrier.sem is None:
            carrier.sem = ("dma", self.n_dma_sems)
            self.n_dma_sems += 1
            self.dma_bufs.append(carrier)
        carrier.semval += 16
        tok = (carrier.sem, carrier.semval)
        self.ops[q].append((_freeze(fn), waits, (carrier.sem, 16)))
        self._upd(tok, reads, writes)

    def barrier(self):
        toks = [(e, self.cnt[e]) for e in ENGS if e != "sp" and self.cnt[e] > 0]
        toks += [(b.sem, b.semval) for b in self.dma_bufs if b.semval > 0]
        for e in ENGS:
            waits = self._waits(e, [t for t in toks if t[0] != e])
            if waits:
                self.ops[e].append((None, waits, None))

    def emit(self, nc, stack):
        sems = {}
        for e in ENGS:
            if e != "sp":
                sems[e] = stack.enter_context(nc.semaphore("s_" + e))
        for i in range(self.n_dma_sems):
            sems[("dma", i)] = stack.enter_context(nc.semaphore("s_dma%d" % i))
        block = stack.enter_context(nc.Block())

        def run(engname):
            def body(engine):
                for fn, waits, inc in self.ops[engname]:
                    for k, v in waits:
                        engine.wait_ge(sems[k], v)
                    if fn is not None:
                        fn(engine).then_inc(sems[inc[0]], inc[1])
            return body

        block.tensor(run("pe"))
        block.scalar(run("act"))
        block.vector(run("dve"))
        block.gpsimd(run("pool"))
        block.sync(run("sp"))


def t5_thresholds():
    n = np.arange(0, 512)
    nf = np.maximum(n, 1).astype(np.float32)
    large = 16 + (np.log(nf / np.float32(16)) / np.float32(math.log(128 / 16)) * np.float32(16)).astype(np.int32)
    large = np.minimum(large, 31)
    b = np.where(n < 16, n, large)
    return [int(np.min(n[b >= k])) for k in range(1, 32)]


def build(S):
    NCH = S // 512
    NSL = NCH // 2
    TO = NSL * 512
    NKB = S // 128
    NT1 = S // 512
    QWK = S // 4
    QWQ = TO // 4
    assert QWQ >= 512
    SCALE = float((64 + 32) ** -0.5)

    nc = bass.Bass("TRN2", target_bir_lowering=False)

    def din(name, shape, dt=F32):
        return nc.dram_tensor(name, list(shape), dt, kind="ExternalInput").ap()

    xT_full = din("xT_full", [D_MODEL, S])
    xT_own = din("xT_own", [D_MODEL, NSL * 640])
    x_own = din("x_own", [TO, D_MODEL])
    pos_full = din("pos_full", [1, S], I32)
    pos_own = din("pos_own", [1, TO], I32)
    pos_bias = din("pos_bias", [1, 256], I32)
    valid_in = din("valid", [128, NSL])
    masks_in = din("masks", [16, 128, 512])
    consts_in = din("consts", [128, 8])
    g1_in = din("g1", [128, 8])
    gq_in = din("gq", [128, 2])
    gk_in = din("gk", [128, 1])
    w1_in = din("w1", [D_MODEL, 320])
    w2_in = din("w2", [D_MODEL, 256])
    w3_in = din("w3", [D_MODEL, 1792])
    wqb_in = din("wqb", [256, 768])
    wqbs_in = din("wqbs", [256, 768])
    wkvb_in = din("wkvb", [128, 1024])
    wo_in = din("wo", [D_MODEL, D_MODEL])
    sinks_in = din("sinks", [1, 8])
    rb_in = din("rb", [1, 256])
    fg_in = din("fg", [1, D_MODEL])
    out_d = nc.dram_tensor("out", [TO, D_MODEL], F32, kind="ExternalOutput").ap()

    P = Prog()
    uid = [0]

    def SB(name, shape, dt, off):
        assert off % 32 == 0, (name, off)
        nb = int(np.prod(shape[1:])) * (2 if dt == BF16 else 4)
        assert SBUF_BASE <= off and off + nb <= SBUF_END, (name, off, nb)
        uid[0] += 1
        return nc.alloc_sbuf_tensor_at("%s_%d" % (name, uid[0]), list(shape), dt, offset=off)

    class Arena:
        def __init__(self, start, end):
            self.p = start
            self.end = end

        def get(self, name, shape, dt):
            nb = int(np.prod(shape[1:])) * (2 if dt == BF16 else 4)
            nb = (nb + 31) // 32 * 32
            t = SB(name, shape, dt, self.p)
            self.p += nb
            assert self.p <= self.end, (name, self.p, self.end)
            return t

    o = SBUF_BASE
    CONST0 = o; o += 20 * 1024
    BIAS0 = o; o += 8 * 1024
    W0 = o; o += 81 * 1024
    KVN0 = o; o += S * 2
    KT0 = o; o += S * 2
    QT0 = o; o += 8 * TO * 2
    QTEND = o
    assert o <= SBUF_END, o
    W1END = KVN0

    with ExitStack() as st:
        PSX = [st.enter_context(nc.psum_tensor("psx%d" % i, [128, 1024], F32)) for i in range(4)]

        def bank(i):
            return PSX[i // 2][:, (i % 2) * 512:(i % 2) * 512 + 512]

        def bank_t(i):
            return PSX[i // 2], (i % 2) * 512

        PB = [Buf("psb%d" % i) for i in range(8)]

        ca = Arena(CONST0, BIAS0)
        CONSTS = ca.get("consts", [128, 8], F32)
        G1 = ca.get("g1", [128, 8], F32)
        GQ = ca.get("gq", [128, 2], F32)
        GK = ca.get("gk", [128, 1], F32)
        EPSC = ca.get("eps", [128, 1], F32)
        PIH = ca.get("pih", [128, 1], F32)
        ESK = ca.get("esk", [128, 8], F32)
        VALID = ca.get("valid", [128, NSL], F32)
        IDENT = ca.get("ident", [128, 128], BF16)
        ONES = ca.get("ones", [128, 128], BF16)
        TQK = ca.get("tqk", [128, 128], F32)
        WKVB = ca.get("wkvb", [128, 1024], BF16)
        WQB = ca.get("wqb", [128, 2, 768], BF16)
        WQBS = ca.get("wqbs", [128, 2, 768], BF16)
        RB = ca.get("rb", [128, 32, 8], F32)
        DIFF = ca.get("diff", [128, 31, 8], F32)
        FG = ca.get("fg", [128, D_MODEL], F32)
        bC = Buf("consts")

        YT = SB("yt", [128, 4, TO], BF16, W0)
        bYT = [Buf("yt%d" % j) for j in range(NSL)]
        BIAS = [SB("bias%d" % t, [128, 8, 128], F32, BIAS0 + t * 4096) for t in range(2)]
        bBIAS = Buf("bias")
        KVN = SB("kvn", [128, S], BF16, KVN0)
        bKVN = Buf("kvn")
        KT = SB("kt", [128, S], BF16, KT0)
        bKTt = [Buf("ktn%d" % t) for t in range(NT1)]
        bKTr = Buf("ktr")
        QT = SB("qt", [128, 8, TO], BF16, QT0)
        bQT = [Buf("qt%d" % h) for h in range(8)]

        wa = Arena(W0, W1END)
        XT0 = wa.get("xt", [128, 8, 512], F32)
        HTs = [wa.get("ht%d" % i, [128, 8, 512], BF16) for i in range(2)]; bHTs = [Buf("ht0"), Buf("ht1")]
        bXTs = [Buf("xt0"), Buf("xt1")]
        RS = wa.get("rs", [128, 512], F32); bRS = Buf("rs")
        LNT = wa.get("lnt", [128, 512], F32); bLNT = Buf("lnt")
        SQ2 = wa.get("sq2", [128, 2, 512], BF16); bSQ2 = Buf("sq2")
        RS2 = wa.get("rs2", [128, 512], F32); bRS2 = Buf("rs2")
        LNT2 = wa.get("lnt2", [128, 512], F32); bLNT2 = Buf("lnt2")
        T1 = wa.get("t1", [128, 512], F32); bT1 = Buf("t1")
        T2 = wa.get("t2", [128, 512], F32); bT2 = Buf("t2")
        QN = wa.get("qn", [128, 2, 512], BF16); bQN = Buf("qn")
        W2 = wa.get("w2", [128, 8, 256], BF16)
        QTAB = [wa.get("qtab%d" % i, [128, QWQ], F32) for i in range(2)]
        stg_off = wa.p
        XT1 = wa.get("xt1", [128, 8, 512], F32)
        XTs = [XT0, XT1]
        STG = [SB("stg%d" % i, [128, 1792], F32, stg_off + i * 7168) for i in range(2)]
        bSTG = [Buf("stg0"), Buf("stg1")]
        bQTAB = Buf("qtab")
        qa = Arena(QT0, SBUF_END)
        W1 = qa.get("w1", [128, 8, 320], BF16)
        KTAB = [qa.get("ktab%d" % i, [128, QWK], F32) for i in range(2)]
        bKTAB = Buf("ktab")
        POSI = qa.get("posi", [128, QWK], I32)
        POSF = qa.get("posf", [128, QWK], F32)
        ANG = qa.get("ang", [128, QWK], F32)
        KI = qa.get("ki", [128, QWK], I32)
        KF = qa.get("kf", [128, QWK], F32)
        bPOSI, bPOSF, bANG, bKI, bKF = Buf("posi"), Buf("posf"), Buf("ang"), Buf("ki"), Buf("kf")
        TI = qa.get("ti", [128, 128], I32)
        bBW = Buf("biaswork")

        P.dma(lambda e: e.dma_start(out=CONSTS[:], in_=consts_in), writes=[bC])
        P.dma(lambda e: e.dma_start(out=G1[:], in_=g1_in), writes=[bC])
        P.dma(lambda e: e.dma_start(out=GQ[:], in_=gq_in), writes=[bC])
        P.dma(lambda e: e.dma_start(out=GK[:], in_=gk_in), writes=[bC])
        P.dma(lambda e: e.dma_start(out=VALID[:], in_=valid_in), writes=[bC])
        P.dma(lambda e: e.dma_start(out=ESK[:], in_=sinks_in.partition_broadcast(128)), writes=[bC])
        P.dma(lambda e: e.dma_start(out=RB[:], in_=rb_in.partition_broadcast(128)), writes=[bC])
        P.dma(lambda e: e.dma_start(out=FG[:], in_=fg_in.partition_broadcast(128)), writes=[bC])
        P.op("pool", lambda e: e.memset(EPSC[:], EPS), writes=[bC])
        P.op("pool", lambda e: e.memset(PIH[:], math.pi / 2), writes=[bC])
        P.op("pool", lambda e: e.memset(ONES[:], 1.0), writes=[bC])
        P.op("pool", lambda e: e.iota(TI[:], pattern=[[1, 128]], base=0, channel_multiplier=-1), writes=[bBW])
        P.op("pool", lambda e: e.tensor_copy(out=TQK[:], in_=TI[:]), reads=[bBW], writes=[bC])
        P.op("pool", lambda e: e.tensor_scalar(out=IDENT[:], in0=TQK[:], scalar1=0.0, scalar2=None, op0=ALU.is_equal),
             reads=[bC], writes=[bC])
        P.op("act", lambda e: e.activation(out=ESK[:], in_=ESK[:], func=AF.Exp), reads=[bC], writes=[bC])

        stg_i = [0]

        def load_w(src_ap, ncols, dst_ap, gain_col, eng="dve"):
            i = stg_i[0] % 2
            stg_i[0] += 1
            sv = STG[i][:, 0:ncols]
            P.dma(lambda e: e.dma_start(out=sv, in_=src_ap), writes=[bSTG[i]])
            P.op(eng, lambda e: e.tensor_scalar(out=dst_ap, in0=sv, scalar1=gain_col, scalar2=None, op0=ALU.mult),
                 reads=[bSTG[i], bC], writes=[bC])

        for kc in range(NKC):
            load_w(w1_in[kc * 128:(kc + 1) * 128, :], 320, W1[:, kc, :], G1[:, kc:kc + 1])
        load_w(wkvb_in, 1024, WKVB[:], GK[:, 0:1])

        def rope_tables(pos_ap, QW, TAB, bTAB):
            for q in range(4):
                P.dma(lambda e, q=q: e.dma_start(out=POSI[32 * q:32 * q + 32, 0:QW],
                                                 in_=pos_ap[:, q * QW:(q + 1) * QW].partition_broadcast(32)),
                      writes=[bPOSI])
            P.op("dve", lambda e: e.tensor_copy(out=POSF[:, 0:QW], in_=POSI[:, 0:QW]), reads=[bPOSI], writes=[bPOSF])
            A_, KI_, KF_ = ANG[:, 0:QW], KI[:, 0:QW], KF[:, 0:QW]
            for ti in range(2):
                if ti == 0:
                    P.op("dve", lambda e: e.tensor_scalar(out=A_, in0=POSF[:, 0:QW], scalar1=CONSTS[:, 0:1], scalar2=PIH[:, 0:1],
                                                          op0=ALU.mult, op1=ALU.add), reads=[bPOSF, bC], writes=[bANG])
                else:
                    P.op("dve", lambda e: e.tensor_scalar(out=A_, in0=POSF[:, 0:QW], scalar1=CONSTS[:, 0:1], scalar2=None,
                                                          op0=ALU.mult), reads=[bPOSF, bC], writes=[bANG])
                P.op("dve", lambda e: e.tensor_scalar(out=KI_, in0=A_, scalar1=1.0 / TWO_PI, scalar2=None, op0=ALU.mult),
                     reads=[bANG], writes=[bKI])
                P.op("dve", lambda e: e.tensor_copy(out=KF_, in_=KI_), reads=[bKI], writes=[bKF])
                P.op("dve", lambda e: e.scalar_tensor_tensor(out=A_, in0=KF_, scalar=-C1, in1=A_, op0=ALU.mult, op1=ALU.add),
                     reads=[bKF, bANG], writes=[bANG])
                P.op("dve", lambda e: e.scalar_tensor_tensor(out=A_, in0=KF_, scalar=-C2, in1=A_, op0=ALU.mult, op1=ALU.add),
                     reads=[bKF, bANG], writes=[bANG])
                P.op("dve", lambda e: e.tensor_scalar(out=KF_, in0=A_, scalar1=math.pi, scalar2=-TWO_PI, op0=ALU.is_gt, op1=ALU.mult),
                     reads=[bANG], writes=[bKF])
                P.op("dve", lambda e: e.tensor_tensor(out=A_, in0=A_, in1=KF_, op=ALU.add), reads=[bANG, bKF], writes=[bANG])
                P.op("dve", lambda e: e.tensor_scalar(out=KF_, in0=A_, scalar1=-math.pi, scalar2=TWO_PI, op0=ALU.is_lt, op1=ALU.mult),
                     reads=[bANG], writes=[bKF])
                P.op("dve", lambda e: e.tensor_tensor(out=A_, in0=A_, in1=KF_, op=ALU.add), reads=[bANG, bKF], writes=[bANG])
                P.op("dve", lambda e: e.tensor_scalar(out=A_, in0=A_, scalar1=3.14159, scalar2=-3.14159, op0=ALU.min, op1=ALU.max),
                     reads=[bANG], writes=[bANG])
                if ti == 0:
                    P.op("act", lambda e: e.activation(out=TAB[0][:, 0:QW], in_=A_, func=AF.Sin), reads=[bANG], writes=[bTAB])
                else:
                    P.op("act", lambda e: e.activation(out=TAB[1][:, 0:QW], in_=A_, func=AF.Sin, scale=CONSTS[:, 1:2]),
                         reads=[bANG, bC], writes=[bTAB])

        for kc in range(NKC):
            load_w(w2_in[kc * 128:(kc + 1) * 128, :], 256, W2[:, kc, :], G1[:, kc:kc + 1])
        for a in range(2):
            load_w(wqb_in[a * 128:(a + 1) * 128, :], 768, WQB[:, a, :], GQ[:, a:a + 1])
            load_w(wqbs_in[a * 128:(a + 1) * 128, :], 768, WQBS[:, a, :], GQ[:, a:a + 1])
        rope_tables(pos_full, QWK, KTAB, bKTAB)
        P.barrier()
        if STOP_AFTER == 'p0':
            P.emit(nc, st)
            return nc
        rope_tables(pos_own, QWQ, QTAB, bQTAB)

        def norm_tile(src_ap, XTt, bXTt, HTt, bHTt, RSt, LNTt, ntok, ss_psx, bss):
            P.dma(lambda e: e.dma_start(out=XTt[:, :, 0:ntok], in_=src_ap.rearrange("(kc p) t -> p kc t", p=128)), writes=[bXTt])
            P.op("act", lambda e: e.activation(out=HTt[:, :, 0:ntok], in_=XTt[:, :, 0:ntok], func=AF.Square), reads=[bXTt], writes=[bHTt])
            for (a, b_) in ((0, min(ntok, 512)), (512, ntok)):
                if b_ <= a:
                    continue
                for kc in range(NKC):
                    P.op("pe", lambda e, kc=kc, a=a, b_=b_: e.matmul(ss_psx[:, a:b_], lhsT=ONES[:], rhs=HTt[:, kc, a:b_],
                                                                   start=(kc == 0), stop=(kc == NKC - 1)),
                         reads=[bHTt, bC], writes=bss)
            P.op("act", lambda e: e.activation(out=LNTt[:, 0:ntok], in_=ss_psx[:, 0:ntok], func=AF.Ln, bias=EPSC[:, 0:1], scale=1.0 / D_MODEL),
                 reads=bss + [bC], writes=[bLNT])
            P.op("act", lambda e: e.activation(out=RSt[:, 0:ntok], in_=LNTt[:, 0:ntok], func=AF.Exp, scale=-0.5), reads=[bLNT], writes=[bRS])
            P.op("dve", lambda e: e.tensor_tensor(out=HTt[:, :, 0:ntok], in0=XTt[:, :, 0:ntok],
                                                  in1=RSt[:, 0:ntok].unsqueeze(1).to_broadcast([128, 8, ntok]), op=ALU.mult),
                 reads=[bXTt, bRS], writes=[bHTt])

        def p1_norm(t):
            norm_tile(xT_full[:, t * 512:t * 512 + 512], XTs[t % 2], bXTs[t % 2], HTs[t % 2], bHTs[t % 2], RS, LNT, 512, PSX[0], [PB[0]])

        def p1_main(t):
            c0 = t * 512
            HT = HTs[t % 2]
            bHT = bHTs[t % 2]
            for (bk, lo, hi, M) in ((2, 0, 128, 128), (3, 128, 224, 96), (4, 224, 320, 96)):
                for kc in range(NKC):
                    P.op("pe", lambda e, kc=kc, bk=bk, lo=lo, hi=hi, M=M: e.matmul(bank(bk)[0:M, :], lhsT=W1[:, kc, lo:hi], rhs=HT[:, kc, :],
                                                                                start=(kc == 0), stop=(kc == NKC - 1)),
                         reads=[bHT, bC], writes=[PB[bk]])
            P.op("act", lambda e: e.activation(out=SQ2[:, 0, :], in_=bank(2), func=AF.Square), reads=[PB[2]], writes=[bSQ2])
            P.op("pe", lambda e: e.matmul(bank(5), lhsT=ONES[:], rhs=SQ2[:, 0, :], start=True, stop=True), reads=[bSQ2, bC], writes=[PB[5]])
            P.op("act", lambda e: e.activation(out=LNT2[:], in_=bank(5), func=AF.Ln, bias=EPSC[:, 0:1], scale=1.0 / 128), reads=[PB[5], bC], writes=[bLNT2])
            P.op("act", lambda e: e.activation(out=RS2[:], in_=LNT2[:], func=AF.Exp, scale=-0.5), reads=[bLNT2], writes=[bRS2])
            P.op("dve", lambda e, c0=c0: e.tensor_tensor(out=KVN[:, c0:c0 + 512], in0=bank(2), in1=RS2[:], op=ALU.mult),
                 reads=[PB[2], bRS2], writes=[bKVN])
            q = c0 // QWK
            off = c0 % QWK
            P.op("dve", lambda e, q=q, off=off: e.tensor_tensor(out=T1[64:96, :], in0=bank(3)[64:96, :], in1=KTAB[0][32 * q:32 * q + 32, off:off + 512], op=ALU.mult),
                 reads=[PB[3], bKTAB], writes=[bT1])
            P.op("dve", lambda e, q=q, off=off: e.tensor_tensor(out=T2[64:96, :], in0=bank(4)[64:96, :], in1=KTAB[1][32 * q:32 * q + 32, off:off + 512], op=ALU.mult),
                 reads=[PB[4], bKTAB], writes=[bT2])
            P.op("dve", lambda e, c0=c0: e.tensor_tensor(out=KT[64:96, c0:c0 + 512], in0=T1[64:96, :], in1=T2[64:96, :], op=ALU.add),
                 reads=[bT1, bT2], writes=[bKTr])

        p1_norm(0)
        for t in range(NT1):
            if t + 1 < NT1:
                p1_norm(t + 1)
            p1_main(t)

        P.barrier()
        if STOP_AFTER == 'p1':
            P.emit(nc, st)
            return nc

        def p2_norm(t):
            norm_tile(xT_own[:, t * 640 + 128:t * 640 + 640], XTs[t % 2], bXTs[t % 2], HTs[t % 2], bHTs[t % 2], RS, LNT, 512, PSX[0], [PB[0]])

        def p2_main(t):
            c0 = t * 512
            HT = HTs[t % 2]
            bHT = bHTs[t % 2]
            for m in range(2):
                for kc in range(NKC):
                    P.op("pe", lambda e, kc=kc, m=m: e.matmul(bank(2 + m), lhsT=W2[:, kc, m * 128:(m + 1) * 128], rhs=HT[:, kc, :],
                                                              start=(kc == 0), stop=(kc == NKC - 1)),
                         reads=[bHT, bC], writes=[PB[2 + m]])
            for m in range(2):
                P.op("act", lambda e, m=m: e.activation(out=SQ2[:, m, :], in_=bank(2 + m), func=AF.Square), reads=[PB[2 + m]], writes=[bSQ2])
            for m in range(2):
                P.op("pe", lambda e, m=m: e.matmul(bank(5), lhsT=ONES[:], rhs=SQ2[:, m, :], start=(m == 0), stop=(m == 1)), reads=[bSQ2, bC], writes=[PB[5]])
            P.op("act", lambda e: e.activation(out=LNT2[:], in_=bank(5), func=AF.Ln, bias=EPSC[:, 0:1], scale=1.0 / 256), reads=[PB[5], bC], writes=[bLNT2])
            P.op("act", lambda e: e.activation(out=RS2[:], in_=LNT2[:], func=AF.Exp, scale=-0.5), reads=[bLNT2], writes=[bRS2])
            for m in range(2):
                P.op("dve", lambda e, m=m: e.tensor_tensor(out=QN[:, m, :], in0=bank(2 + m), in1=RS2[:], op=ALU.mult),
                     reads=[PB[2 + m], bRS2], writes=[bQN])
            q = c0 // QWQ
            off = c0 % QWQ
            for h in range(8):
                bq = 6 + (h % 2)
                bs = 4 if h % 2 == 0 else 1
                for m in range(2):
                    P.op("pe", lambda e, m=m, h=h, bq=bq: e.matmul(bank(bq)[0:96, :], lhsT=WQB[:, m, h * 96:(h + 1) * 96], rhs=QN[:, m, :],
                                                                   start=(m == 0), stop=(m == 1)), reads=[bQN, bC], writes=[PB[bq]])
                for m in range(2):
                    P.op("pe", lambda e, m=m, h=h, bs=bs: e.matmul(bank(bs)[0:96, :], lhsT=WQBS[:, m, h * 96:(h + 1) * 96], rhs=QN[:, m, :],
                                                                   start=(m == 0), stop=(m == 1)), reads=[bQN, bC], writes=[PB[bs]])
                P.op("act", lambda e, h=h, bq=bq, c0=c0: e.activation(out=QT[0:64, h, c0:c0 + 512], in_=bank(bq)[0:64, :], func=AF.Copy, scale=SCALE),
                     reads=[PB[bq]], writes=[bQT[h]])
                P.op("dve", lambda e, bq=bq, q=q, off=off: e.scalar_tensor_tensor(out=T1[64:96, :], in0=bank(bq)[64:96, :], scalar=SCALE,
                                                                                in1=QTAB[0][32 * q:32 * q + 32, off:off + 512], op0=ALU.mult, op1=ALU.mult),
                     reads=[PB[bq], bQTAB], writes=[bT1])
                P.op("dve", lambda e, bs=bs, q=q, off=off: e.scalar_tensor_tensor(out=T2[64:96, :], in0=bank(bs)[64:96, :], scalar=SCALE,
                                                                                in1=QTAB[1][32 * q:32 * q + 32, off:off + 512], op0=ALU.mult, op1=ALU.mult),
                     reads=[PB[bs], bQTAB], writes=[bT2])
                P.op("dve", lambda e, h=h, c0=c0: e.tensor_tensor(out=QT[64:96, h, c0:c0 + 512], in0=T1[64:96, :], in1=T2[64:96, :], op=ALU.add),
                     reads=[bT1, bT2], writes=[bQT[h]])

        p2_norm(0)
        for t in range(NSL):
            if t + 1 < NSL:
                p2_norm(t + 1)
            p2_main(t)

        P.barrier()
        if STOP_AFTER == 'p2':
            P.emit(nc, st)
            return nc

        pa = Arena(W0 + 4 * TO * 2, W1END)
        VP = pa.get("vp", [128, NKB, 128], BF16); bVPg = [Buf("vp%d" % g) for g in range(NKB // 8)]
        MASKT = pa.get("maskt", [128, 16, 512], BF16); bMASK = Buf("mask")
        PT = [pa.get("pt%d" % i, [128, 512], BF16) for i in range(4)]
        bPT = [Buf("pt%d" % i) for i in range(4)]
        RC = pa.get("rc", [128, 512], F32); bRC = Buf("rc")
        tmpb_off = pa.p
        TMPB = pa.get("tmpb", [128, 8, 128], F32)
        MSTG = [SB("mstg%d" % i, [128, 512], F32, tmpb_off + i * 2048) for i in range(2)]
        bMSTG = [Buf("mstg0"), Buf("mstg1")]
        for i in range(16):
            k = i % 2
            P.dma(lambda e, i=i, k=k: e.dma_start(out=MSTG[k][:], in_=masks_in[i]), writes=[bMSTG[k]])
            P.op("dve", lambda e, i=i, k=k: e.tensor_copy(out=MASKT[:, i, :], in_=MSTG[k][:]), reads=[bMSTG[k]], writes=[bMASK])

        PB256 = pa.get("pb256", [128, 256], I32)
        PQF = pa.get("pqf", [128, 128], F32)
        KCOL = pa.get("kcol", [128, 2], I32)
        KCOLF = pa.get("kcolf", [128, 2], F32)
        DT_ = [pa.get("dt%d" % t, [128, 128], F32) for t in range(2)]
        IND = pa.get("ind", [128, 128], F32)
        MSK = pa.get("msk", [128, 128], F32)
        thr = t5_thresholds()
        P.dma(lambda e: e.dma_start(out=PB256[:], in_=pos_bias.partition_broadcast(128)), writes=[bBW])
        P.dma(lambda e: e.dma_start(out=KCOL[:, 0:1], in_=pos_bias[0:1, 0:128].rearrange("o (p f) -> p (o f)", f=1)), writes=[bBW])
        P.dma(lambda e: e.dma_start(out=KCOL[:, 1:2], in_=pos_bias[0:1, 128:256].rearrange("o (p f) -> p (o f)", f=1)), writes=[bBW])
        P.op("pool", lambda e: e.tensor_copy(out=PQF[:], in_=PB256[:, 128:256]), reads=[bBW], writes=[bBW])
        P.op("pool", lambda e: e.tensor_copy(out=KCOLF[:], in_=KCOL[:]), reads=[bBW], writes=[bBW])
        P.op("pool", lambda e: e.tensor_tensor(out=DIFF[:], in0=RB[:, 1:32, :], in1=RB[:, 0:31, :], op=ALU.subtract), reads=[bC, bBW], writes=[bBW])
        for t in range(2):
            P.op("pool", lambda e, t=t: e.tensor_scalar(out=DT_[t][:], in0=PQF[:], scalar1=KCOLF[:, t:t + 1], scalar2=None, op0=ALU.subtract),
                 reads=[bBW], writes=[bBW])
            P.op("pool", lambda e, t=t: e.tensor_copy(out=BIAS[t][:], in_=RB[:, 0, :].unsqueeze(2).to_broadcast([128, 8, 128])),
                 reads=[bC], writes=[bBIAS])
            for b in range(31):
                P.op("pool", lambda e, t=t, b=b: e.tensor_scalar(out=IND[:], in0=DT_[t][:], scalar1=float(thr[b]), scalar2=None, op0=ALU.is_ge),
                     reads=[bBW], writes=[bBW])
                P.op("pool", lambda e, b=b: e.tensor_tensor(out=TMPB[:], in0=IND[:].unsqueeze(1).to_broadcast([128, 8, 128]),
                                                            in1=DIFF[:, b, :].unsqueeze(2).to_broadcast([128, 8, 128]), op=ALU.mult),
                     reads=[bBW], writes=[bBW, bMSTG[0], bMSTG[1]])
                P.op("pool", lambda e, t=t: e.tensor_tensor(out=BIAS[t][:], in0=BIAS[t][:], in1=TMPB[:], op=ALU.add),
                     reads=[bBW, bBIAS], writes=[bBIAS])
            if t == 0:
                P.op("pool", lambda e: e.tensor_scalar(out=MSK[:], in0=TQK[:], scalar1=0.0, scalar2=NEG, op0=ALU.is_ge, op1=ALU.mult),
                     reads=[bC], writes=[bBW])
            else:
                P.op("pool", lambda e: e.tensor_scalar(out=MSK[:], in0=TQK[:], scalar1=0.0, scalar2=NEG, op0=ALU.is_lt, op1=ALU.mult),
                     reads=[bC], writes=[bBW])
            P.op("pool", lambda e, t=t: e.tensor_tensor(out=BIAS[t][:], in0=BIAS[t][:], in1=MSK[:].unsqueeze(1).to_broadcast([128, 8, 128]), op=ALU.add),
                 reads=[bBW, bBIAS], writes=[bBIAS])

        W3 = SB("w3", [128, 8, 1792], BF16, QT0)
        WO = SB("wo", [128, 8, 1024], BF16, QT0 + 8 * 1792 * 2)
        WSTG0 = QT0 + 8 * 1792 * 2 + 8 * 1024 * 2
        WSTG = [SB("wstg%d" % i, [128, 1024], F32, WSTG0 + i * 4096) for i in range(3)]
        bWSTG = [Buf("wstg%d" % i) for i in range(3)]
        W3END = WSTG0 + 3 * 4096
        assert W3END <= SBUF_END
        bW3 = Buf("w3")
        n_ov = min(8, -(-(W3END - QT0) // (TO * 2)))

        def load_p3_weights():
            dead = [bQT[hh] for hh in range(n_ov)]
            jobs = []
            for kc in range(NKC):
                jobs.append((w3_in[kc * 128:(kc + 1) * 128, 0:896], W3[:, kc, 0:896], G1[:, kc:kc + 1], 896))
                jobs.append((w3_in[kc * 128:(kc + 1) * 128, 896:1792], W3[:, kc, 896:1792], G1[:, kc:kc + 1], 896))
            for kc in range(NKC):
                jobs.append((wo_in[kc * 128:(kc + 1) * 128, :], WO[:, kc, :], None, 1024))
            for n, (src, dst, gain, ncol) in enumerate(jobs):
                i = n % 3
                P.dma(lambda e, src=src, i=i, ncol=ncol: e.dma_start(out=WSTG[i][:, 0:ncol], in_=src), writes=[bWSTG[i]] + dead)
                if gain is None:
                    P.op("pool", lambda e, dst=dst, i=i, ncol=ncol: e.tensor_copy(out=dst, in_=WSTG[i][:, 0:ncol]), reads=[bWSTG[i]], writes=[bW3] + dead)
                else:
                    P.op("pool", lambda e, dst=dst, i=i, ncol=ncol, gain=gain: e.tensor_scalar(out=dst, in0=WSTG[i][:, 0:ncol], scalar1=gain, scalar2=None, op0=ALU.mult),
                         reads=[bWSTG[i], bC], writes=[bW3] + dead)

        steps = [(h, j, kb) for h in range(8) for j in range(NSL) for kb in range(8 * j + 8)]
        LA = 3
        NR = 4

        def build_kv(h, jn):
            vo = 64 * (h % 2)
            so = 64 - vo
            for t in (2 * jn, 2 * jn + 1):
                bk = 6 + (t % 2)
                P.op("pe", lambda e, t=t, bk=bk: e.matmul(bank(bk)[0:64, :], lhsT=WKVB[:, h * 128:h * 128 + 64], rhs=KVN[:, t * 512:(t + 1) * 512],
                                                          start=True, stop=True), reads=[bKVN, bC], writes=[PB[bk]])
                P.op("dve", lambda e, t=t, bk=bk: e.tensor_copy(out=KT[0:64, t * 512:(t + 1) * 512], in_=bank(bk)[0:64, :]), reads=[PB[bk]], writes=[bKTt[t]])
            g8 = jn
            P.op("dve", lambda e: e.memset(VP[:, g8 * 8:(g8 + 1) * 8, so:so + 64], 1.0), writes=[bVPg[g8]])
            for bb in range(8):
                blk = g8 * 8 + bb
                P.op("pe", lambda e, blk=blk, bb=bb: e.matmul(bank(7)[:, bb * 64:(bb + 1) * 64], lhsT=KVN[:, blk * 128:(blk + 1) * 128],
                                                              rhs=WKVB[:, h * 128 + 64:h * 128 + 128], start=True, stop=True),
                     reads=[bKVN, bC], writes=[PB[7]])
            P.op("dve", lambda e: e.tensor_copy(out=VP[:, g8 * 8:(g8 + 1) * 8, vo:vo + 64], in_=bank(7).rearrange("p (a b) -> p a b", a=8)),
                 reads=[PB[7]], writes=[bVPg[g8]])

        def emit_qk(idx):
            h, j, kb = steps[idx]
            sb = idx % NR
            P.op("pe", lambda e: e.matmul(bank(sb), lhsT=KT[0:96, kb * 128:(kb + 1) * 128],
                                          rhs=QT[0:96, h, j * 512:(j + 1) * 512], start=True, stop=True),
                 reads=[bKTt[kb // 4], bKTr, bQT[h]], writes=[PB[sb]])

        build_kv(0, 0)
        for idx in range(min(LA, len(steps))):
            emit_qk(idx)
        XH = 16 if NSL >= 2 else 4
        for idx, (h, j, kb) in enumerate(steps):
            vo = 64 * (h % 2)
            so = 64 - vo
            if h == n_ov and j == 0 and kb == 0:
                load_p3_weights()
            if j + 1 < NSL and kb == max(0, 8 * j + 8 - 12):
                build_kv(h, j + 1)
            if j == NSL - 1 and kb == XH and h + 1 < 8:
                build_kv(h + 1, 0)
            if idx + LA < len(steps):
                emit_qk(idx + LA)
            sb = idx % NR
            ob = 4 + ((h * NSL + j) % 2)
            last = 8 * j + 7
            P.op("act", lambda e: e.activation(out=PT[sb][:], in_=bank(sb), func=AF.Exp), reads=[PB[sb]], writes=[bPT[sb]])
            if kb >= 8 * j:
                mi = (j % 2) * 8 + (kb - 8 * j)
                P.op("dve", lambda e: e.tensor_tensor(out=PT[sb][:], in0=PT[sb][:], in1=MASKT[:, mi, :], op=ALU.mult),
                     reads=[bPT[sb], bMASK], writes=[bPT[sb]])
            P.op("pe", lambda e: e.matmul(bank(ob), lhsT=VP[:, kb, :], rhs=PT[sb][:], start=(kb == 0), stop=(kb == last)),
                 reads=[bVPg[kb // 8], bPT[sb]], writes=[PB[ob]])
            if kb == last:
                P.op("dve", lambda e: e.reciprocal(out=RC[so:so + 64, :], in_=bank(ob)[so:so + 64, :]), reads=[PB[ob]], writes=[bRC])
                P.op("dve", lambda e: e.tensor_tensor(out=YT[vo:vo + 64, h // 2, j * 512:(j + 1) * 512],
                                                      in0=bank(ob)[vo:vo + 64, :], in1=RC[so:so + 64, :], op=ALU.mult),
                     reads=[PB[ob], bRC], writes=[bYT[j]])

        if n_ov >= 8:
            load_p3_weights()
        P.barrier()
        if STOP_AFTER == 'p2b':
            P.emit(nc, st)
            return nc

        p3 = Arena(W0 + 4 * TO * 2, QT0)
        p3b = Arena(WSTG0, SBUF_END)
        xt3_off = p3.p
        XT3 = p3.get("xt3", [128, 8, 640], F32)
        HT3s = [p3.get("ht3_%d" % i, [128, 8, 640], BF16) for i in range(2)]; bHT3s = [Buf("ht3_0"), Buf("ht3_1")]
        gt1_off = p3.p
        GT1 = [p3.get("gt1_%d" % i, [128, 512], F32) for i in range(2)]; bGT1 = [Buf("gt1_0"), Buf("gt1_1")]
        RS3 = p3.get("rs3", [128, 640], F32)
        LNT3 = RS3
        bXT3 = Buf("xt3")
        QS = p3b.get("qs", [128, 4, 512], BF16); bQS = Buf("qs")
        KS = p3b.get("ks", [128, 640], BF16); bKS = Buf("ks")
        VS = p3b.get("vs", [128, 5, 2, 128], BF16); bVS = Buf("vs")
        PTS = [[p3.get("pts%d_%d" % (p_, i), [128, 512], BF16) for i in range(2)] for p_ in range(2)]
        bPTS = [[Buf("pts%d_%d" % (p_, i)) for i in range(2)] for p_ in range(2)]
        RCS = [p3.get("rcs%d" % p_, [128, 512], F32) for p_ in range(2)]; bRCS = [Buf("rcs0"), Buf("rcs1")]
        YS = p3b.get("ys", [128, 4, 512], BF16); bYS = Buf("ys")
        SGs = [p3.get("sg%d" % i, [128, 8, 512], BF16) for i in range(2)]; bSGs = [Buf("sg0"), Buf("sg1")]
        XO = [p3.get("xo%d" % i, [128, 1024], F32) for i in range(2)]; bXO = [Buf("xo0"), Buf("xo1")]
        RR = p3b.get("rr", [128, 1024], F32); bRR = Buf("rr")
        JNK = SB("jnk", [128, 1024], BF16, gt1_off); bJNK = bGT1[0]
        ONEC = p3b.get("onec", [128, 1], F32)
        OB1 = p3b.get("ob", [128, 1024], F32); OB = [OB1, OB1]; bOB1 = Buf("ob"); bOB = [bOB1, bOB1]
        SS4 = p3b.get("ss4", [128, 4], F32); bSS4 = Buf("ss4")
        BHLT = SB("bhlt", [128, 8, 512], BF16, xt3_off)
        BDT = SB("bdt", [128, 512], F32, xt3_off + 8192)
        BHL = SB("bhl", [128, 8, 512], BF16, BIAS0)
        bBHL = Buf("bhl")
        P.op("pool", lambda e: e.memset(VS[:, :, :, 64:128], 1.0), writes=[bVS])
        P.op("pool", lambda e: e.memset(ONEC[:], 1.0), writes=[bC])
        for t in range(2):
            for g in range(2):
                k = t * 2 + g
                src = BIAS[t][:, 4 * g:4 * g + 4, :].rearrange("p a b -> p (a b)")
                P.op("dve", lambda e, k=k, src=src: e.tensor_copy(out=BHLT[:, k, :], in_=src), reads=[bBIAS], writes=[bBHL])
                P.op("dve", lambda e, k=k, src=src: e.tensor_tensor(out=BDT[:], in0=src, in1=BHLT[:, k, :], op=ALU.subtract), reads=[bBIAS, bBHL], writes=[bBHL])
                P.op("dve", lambda e, k=k: e.tensor_copy(out=BHLT[:, 4 + k, :], in_=BDT[:]), reads=[bBHL], writes=[bBHL])
        P.barrier()
        P.op("dve", lambda e: e.tensor_copy(out=BHL[:], in_=BHLT[:]), reads=[bBHL], writes=[bBHL])
        P.barrier()
        if STOP_AFTER == 'p3w':
            P.emit(nc, st)
            return nc

        def proj(HT3, bHT3, col0, ncol, bk, ntok=512, tok0=128):
            for kc in range(NKC):
                P.op("pe", lambda e, kc=kc: e.matmul(bank(bk)[0:ncol, 0:ntok], lhsT=W3[:, kc, col0:col0 + ncol], rhs=HT3[:, kc, tok0:tok0 + ntok],
                                                     start=(kc == 0), stop=(kc == NKC - 1)), reads=[bHT3, bW3], writes=[PB[bk]])

        def p3_pre(j):
            norm_tile(xT_own[:, j * 640:(j + 1) * 640], XT3, bXT3, HT3s[j % 2], bHT3s[j % 2], RS3, LNT3, 640, PSX[0], [PB[0], PB[1]])

        def p3_proj(j):
            HT3 = HT3s[j % 2]
            bHT3 = bHT3s[j % 2]
            for u in range(4):
                bk = 2 + (u % 2)
                proj(HT3, bHT3, u * 128, 128, bk)
                P.op("act", lambda e, u=u, bk=bk: e.activation(out=QS[:, u, :], in_=bank(bk), func=AF.Copy, scale=0.125), reads=[PB[bk]], writes=[bQS])
            proj(HT3, bHT3, 512, 128, 4, ntok=512, tok0=128)
            P.op("dve", lambda e: e.tensor_copy(out=KS[:, 128:640], in_=bank(4)), reads=[PB[4]], writes=[bKS])
            proj(HT3, bHT3, 512, 128, 5, ntok=128, tok0=0)
            P.op("dve", lambda e: e.tensor_copy(out=KS[:, 0:128], in_=bank(5)[:, 0:128]), reads=[PB[5]], writes=[bKS])
            for w in range(5):
                for kc in range(NKC):
                    P.op("pe", lambda e, kc=kc, w=w: e.matmul(bank(6)[:, w * 128:(w + 1) * 128] if w < 4 else bank(7)[:, 0:128],
                                                              lhsT=HT3[:, kc, w * 128:(w + 1) * 128], rhs=W3[:, kc, 640:768],
                                                              start=(kc == 0), stop=(kc == NKC - 1)), reads=[bHT3, bW3], writes=[PB[6] if w < 4 else PB[7]])
            P.op("dve", lambda e: e.tensor_copy(out=VS[:, 0:4, :, 0:64], in_=bank(6).rearrange("p (w g d) -> p w g d", w=4, g=2)),
                 reads=[PB[6]], writes=[bVS])
            P.op("dve", lambda e: e.tensor_copy(out=VS[:, 4, :, 0:64], in_=bank(7)[:, 0:128].rearrange("p (g d) -> p g d", g=2)),
                 reads=[PB[7]], writes=[bVS])
            P.op("dve", lambda e: e.tensor_scalar(out=VS[:, 0, :, 64:128], in0=ONES[:].rearrange("p (g d) -> p g d", g=2),
                                                  scalar1=VALID[:, j:j + 1], scalar2=None, op0=ALU.mult), reads=[bC], writes=[bVS])

        def gate_tile(j, gt):
            HT3 = HT3s[j % 2]
            bHT3 = bHT3s[j % 2]
            SG = SGs[j % 2]
            bSG = bSGs[j % 2]
            bk = gt % 2
            gp = gt % 2
            proj(HT3, bHT3, 768 + gt * 128, 128, bk)
            P.op("act", lambda e: e.activation(out=GT1[gp][:], in_=bank(bk), func=AF.Exp, scale=-1.0), reads=[PB[bk]], writes=[bGT1[gp]])
            P.op("act", lambda e: e.activation(out=GT1[gp][:], in_=GT1[gp][:], func=AF.Ln, bias=ONEC[:, 0:1]), reads=[bGT1[gp], bC], writes=[bGT1[gp]])
            P.op("act", lambda e: e.activation(out=GT1[gp][:], in_=GT1[gp][:], func=AF.Exp, scale=-1.0), reads=[bGT1[gp]], writes=[bGT1[gp]])
            P.op("dve", lambda e: e.tensor_tensor(out=SG[:, gt, :], in0=bank(bk), in1=GT1[gp][:], op=ALU.mult), reads=[PB[bk], bGT1[gp]], writes=[bSG])

        def swa_iter(j, n_it):
            i, g = n_it // 2, n_it % 2
            pn = n_it % 2
            ob = 6 + pn
            for t in range(2):
                w = i + t
                sbk = 2 + 2 * pn + t
                k = t * 2 + g
                P.op("pe", lambda e, w=w, sbk=sbk: e.matmul(bank(sbk), lhsT=KS[64 * g:64 * g + 64, w * 128:(w + 1) * 128],
                                                          rhs=QS[64 * g:64 * g + 64, :, i * 128:(i + 1) * 128], start=True, stop=False),
                     reads=[bKS, bQS], writes=[PB[sbk]])
                P.op("pe", lambda e, sbk=sbk, k=k: e.matmul(bank(sbk), lhsT=IDENT[:], rhs=BHL[:, k, :], start=False, stop=False),
                     reads=[bBHL, bC], writes=[PB[sbk]])
                P.op("pe", lambda e, sbk=sbk, k=k: e.matmul(bank(sbk), lhsT=IDENT[:], rhs=BHL[:, 4 + k, :], start=False, stop=True),
                     reads=[bBHL, bC], writes=[PB[sbk]])
                P.op("act", lambda e, t=t, sbk=sbk: e.activation(out=PTS[pn][t][:], in_=bank(sbk), func=AF.Exp), reads=[PB[sbk]], writes=[bPTS[pn][t]])
            for t in range(2):
                w = i + t
                P.op("pe", lambda e, t=t, w=w: e.matmul(bank(ob), lhsT=VS[:, w, g, :], rhs=PTS[pn][t][:], start=(t == 0), stop=(t == 1)),
                     reads=[bVS, bPTS[pn][t]], writes=[PB[ob]])
            P.op("dve", lambda e: e.tensor_tensor(out=RCS[pn][64:128, :].rearrange("p (a b) -> p a b", a=4),
                                                  in0=bank(ob)[64:128, :].rearrange("p (a b) -> p a b", a=4),
                                                  in1=ESK[64:128, 4 * g:4 * g + 4].unsqueeze(2).to_broadcast([64, 4, 128]), op=ALU.add),
                 reads=[PB[ob], bC], writes=[bRCS[pn]])
            P.op("act", lambda e: e.activation(out=RCS[pn][64:128, :], in_=RCS[pn][64:128, :], func=AF.Ln), reads=[bRCS[pn]], writes=[bRCS[pn]])
            P.op("act", lambda e: e.activation(out=RCS[pn][64:128, :], in_=RCS[pn][64:128, :], func=AF.Exp, scale=-1.0), reads=[bRCS[pn]], writes=[bRCS[pn]])
            for u in range(4):
                po = 64 * (u % 2)
                tl = 2 * g + u // 2
                P.op("dve", lambda e, u=u, po=po, tl=tl: e.tensor_tensor(out=YS[po:po + 64, tl, i * 128:(i + 1) * 128],
                                                                         in0=bank(ob)[0:64, u * 128:(u + 1) * 128],
                                                                         in1=RCS[pn][64:128, u * 128:(u + 1) * 128], op=ALU.mult),
                     reads=[PB[ob], bRCS[pn]], writes=[bYS])

        def yg(j):
            SG = SGs[j % 2]
            bSG = bSGs[j % 2]
            P.op("dve", lambda e: e.tensor_tensor(out=SG[:, 0:4, :], in0=YT[:, :, j * 512:(j + 1) * 512], in1=SG[:, 0:4, :], op=ALU.mult),
                 reads=[bYT[j], bSG], writes=[bSG])
            P.op("dve", lambda e: e.tensor_tensor(out=SG[:, 4:8, :], in0=YS[:], in1=SG[:, 4:8, :], op=ALU.mult),
                 reads=[bYS, bSG], writes=[bSG])

        def out_block(j, i):
            SG = SGs[j % 2]
            bSG = bSGs[j % 2]
            row0 = j * 512 + i * 128
            k = i % 2
            P.dma(lambda e: e.dma_start(out=XO[k][:], in_=x_own[row0:row0 + 128, :]), writes=[bXO[k]])
            for hf in range(2):
                bk = hf
                for kc in range(NKC):
                    P.op("pe", lambda e, kc=kc, hf=hf, bk=bk: e.matmul(bank(bk), lhsT=SG[:, kc, i * 128:(i + 1) * 128], rhs=WO[:, kc, hf * 512:(hf + 1) * 512],
                                                                       start=(kc == 0), stop=(kc == NKC - 1)), reads=[bSG, bW3], writes=[PB[bk]])
                P.op("dve", lambda e, hf=hf, bk=bk: e.tensor_tensor(out=RR[:, hf * 512:(hf + 1) * 512], in0=bank(bk), in1=XO[k][:, hf * 512:(hf + 1) * 512], op=ALU.add),
                     reads=[PB[bk], bXO[k]], writes=[bRR])
            P.op("act", lambda e: e.activation(out=JNK[:], in_=RR[:], func=AF.Square, accum_out=SS4[:, 0:1]), reads=[bRR], writes=[bJNK, bSS4])
            P.op("act", lambda e: e.activation(out=SS4[:, 1:2], in_=SS4[:, 0:1], func=AF.Ln, bias=EPSC[:, 0:1], scale=1.0 / D_MODEL), reads=[bSS4, bC], writes=[bSS4])
            P.op("act", lambda e: e.activation(out=SS4[:, 2:3], in_=SS4[:, 1:2], func=AF.Exp, scale=-0.5), reads=[bSS4], writes=[bSS4])
            P.op("dve", lambda e: e.scalar_tensor_tensor(out=OB[k][:], in0=RR[:], scalar=SS4[:, 2:3], in1=FG[:], op0=ALU.mult, op1=ALU.mult),
                 reads=[bRR, bSS4, bC], writes=[bOB[k]])
            P.dma(lambda e: e.dma_start(out=out_d[row0:row0 + 128, :], in_=OB[k][:]), reads=[bOB[k]])

        p3_pre(0)
        p3_proj(0)
        for j in range(NSL + 1):
            if j + 1 < NSL:
                p3_pre(j + 1)
            for n_it in range(8):
                if j < NSL:
                    swa_iter(j, n_it)
                    gate_tile(j, n_it)
                if j >= 1 and n_it % 2 == 1:
                    out_block(j - 1, n_it // 2)
            if j < NSL:
                yg(j)
            if j + 1 < NSL:
                p3_proj(j + 1)

        P.barrier()
        P.emit(nc, st)
    return nc


def chunk_of(j, p):
    return 2 * j + ((j + p) % 2)


def make_masks(p):
    m = np.zeros((2, 8, 128, 512), np.float32)
    k = np.arange(128)[:, None]
    q = np.arange(512)[None, :]
    for s in range(2):
        delta = (s + p) % 2
        for r in range(8):
            vis = (r * 128 + k) <= (delta * 512 + q)
            m[s, r] = np.where(vis, 1.0, 0.0)
    return m.reshape(16, 128, 512)


def prep_inputs(inputs, B, S):
    NSL = S // 1024
    x = np.asarray(inputs["x"], np.float32)
    pos = np.asarray(inputs["positions"], np.int32)
    w_in = np.asarray(inputs["w_in"], np.float32)[0]
    w1 = np.ascontiguousarray(np.concatenate([w_in[:, 256:384], w_in[:, 320:416], w_in[:, 320:384], w_in[:, 400:416], w_in[:, 384:400]], axis=1))
    w2 = np.ascontiguousarray(w_in[:, 0:256])
    qs_cols = []
    for u in range(4):
        qs_cols.append(w_in[:, 416 + u * 64:416 + (u + 1) * 64])
        qs_cols.append(w_in[:, 416 + (4 + u) * 64:416 + (5 + u) * 64])
    w3 = np.ascontiguousarray(np.concatenate(qs_cols + [w_in[:, 928:1056], w_in[:, 1056:1184], w_in[:, 1184:2208]], axis=1))
    wqb = np.asarray(inputs["w_q_b"], np.float32)[0]
    wqbs = wqb.copy()
    for h in range(8):
        b0 = h * 96 + 64
        wqbs[:, b0:b0 + 16] = wqb[:, b0 + 16:b0 + 32]
        wqbs[:, b0 + 16:b0 + 32] = wqb[:, b0:b0 + 16]
    wkvb = np.ascontiguousarray(np.asarray(inputs["w_kv_b"], np.float32)[0])
    wo = np.ascontiguousarray(np.asarray(inputs["w_out"], np.float32)[0])
    g1 = np.ascontiguousarray(np.asarray(inputs["norm_gain"], np.float32)[0].reshape(8, 128).T)
    gq = np.ascontiguousarray(np.asarray(inputs["q_a_norm"], np.float32)[0].reshape(2, 128).T)
    gk = np.ascontiguousarray(np.asarray(inputs["kv_a_norm"], np.float32)[0].reshape(128, 1))
    sinks = np.ascontiguousarray(np.asarray(inputs["sinks"], np.float32)[0].reshape(1, 8))
    rb = np.ascontiguousarray(np.asarray(inputs["rel_bias"], np.float32).reshape(1, 256))
    fg = np.ascontiguousarray(np.asarray(inputs["final_norm"], np.float32).reshape(1, D_MODEL))
    consts = np.zeros((128, 8), np.float32)
    pidx = np.arange(128)
    consts[:, 0] = (10000.0 ** (-((pidx % 16).astype(np.float64)) / 16.0)).astype(np.float32)
    consts[:, 1] = np.where((pidx % 32) < 16, -1.0, 1.0)
    in_maps = []
    for c in range(2 * B):
        b, p = c // 2, c % 2
        xb = x[b]
        xT = np.ascontiguousarray(xb.T)
        xTo = np.zeros((D_MODEL, NSL * 640), np.float32)
        xo = np.zeros((NSL * 512, D_MODEL), np.float32)
        po = np.zeros((1, NSL * 512), np.int32)
        valid = np.ones((128, NSL), np.float32)
        for j in range(NSL):
            ch = chunk_of(j, p)
            lo = ch * 512
            xo[j * 512:(j + 1) * 512] = xb[lo:lo + 512]
            po[0, j * 512:(j + 1) * 512] = pos[b, lo:lo + 512]
            xTo[:, j * 640 + 128:(j + 1) * 640] = xT[:, lo:lo + 512]
            if lo >= 128:
                xTo[:, j * 640:j * 640 + 128] = xT[:, lo - 128:lo]
            else:
                valid[:, j] = 0.0
        c1 = chunk_of(1, p) * 512
        pb = np.ascontiguousarray(pos[b:b + 1, c1 - 128:c1 + 128])
        in_maps.append({
            "xT_full": xT, "xT_own": xTo, "x_own": xo,
            "pos_full": np.ascontiguousarray(pos[b:b + 1]), "pos_own": po, "pos_bias": pb,
            "valid": valid, "masks": make_masks(p), "consts": consts,
            "g1": g1, "gq": gq, "gk": gk, "w1": w1, "w2": w2, "w3": w3,
            "wqb": np.ascontiguousarray(wqb), "wqbs": np.ascontiguousarray(wqbs), "wkvb": wkvb, "wo": wo,
            "sinks": sinks, "rb": rb, "fg": fg,
        })
    return in_maps


def assemble(results, B, S):
    NSL = S // 1024
    out = np.zeros((B, S, D_MODEL), np.float32)
    for c in range(2 * B):
        b, p = c // 2, c % 2
        oc = np.asarray(results[c]["out"])
        for j in range(NSL):
            ch = chunk_of(j, p)
            out[b, ch * 512:(ch + 1) * 512] = oc[j * 512:(j + 1) * 512]
    return out


def run(inputs, B, S):
    nc = build(S)
    in_maps = prep_inputs(inputs, B, S)
    res = run_bass_kernel_spmd(nc, in_maps, core_ids=list(range(2 * B)))
    return assemble(res.results, B, S)


def kernel(**inputs):
    return run(inputs, 4, 8192)
```

```python
import math
import types
from contextlib import ExitStack

import numpy as np
import concourse.bass as bass
import concourse.mybir as mybir
from concourse.bass_utils import run_bass_kernel_spmd

F32 = mybir.dt.float32
BF16 = mybir.dt.bfloat16
I32 = mybir.dt.int32
AF = mybir.ActivationFunctionType
ALU = mybir.AluOpType

D_MODEL = 1024
NKC = 8
IN_WIDTH = 2208
EPS = 1e-6
NEG = -30000.0
SBUF_BASE = 17408
SBUF_END = 229376
ENGS = ("pe", "act", "dve", "pool", "sp")
TWO_PI = 2.0 * math.pi
C1 = 6.28125
C2 = TWO_PI - C1
SKIP = set()
STOP_AFTER = None


class Buf:
    __slots__ = ("name", "w", "r", "sem", "semval")

    def __init__(self, name):
        self.name = name
        self.w = None
        self.r = []
        self.sem = None
        self.semval = 0


def _freeze(fn):
    if fn is None or fn.__closure__ is None:
        return fn
    cells = []
    for c in fn.__closure__:
        try:
            cells.append(types.CellType(c.cell_contents))
        except ValueError:
            cells.append(c)
    return types.FunctionType(fn.__code__, fn.__globals__, fn.__name__, fn.__defaults__, tuple(cells))


class Prog:
    def __init__(self):
        self.ops = {e: [] for e in ENGS}
        self.cnt = {e: 0 for e in ENGS}
        self.seen = {e: {} for e in ENGS}
        self.dma_bufs = []
        self.n_dma_sems = 0

    def _waits(self, eng, deps):
        need = {}
        for (k, v) in deps:
            if k == eng and eng == "pe":
                continue
            if v > need.get(k, 0):
                need[k] = v
        out = []
        seen = self.seen[eng]
        for k, v in need.items():
            if seen.get(k, 0) < v:
                seen[k] = v
                out.append((k, v))
        return out

    @staticmethod
    def _deps(reads, writes):
        deps = []
        for b in reads:
            if b.w is not None:
                deps.append(b.w)
        for b in writes:
            if b.w is not None:
                deps.append(b.w)
            deps.extend(b.r)
        return deps

    @staticmethod
    def _upd(tok, reads, writes):
        for b in reads:
            b.r.append(tok)
            if len(b.r) > 64:
                best = {}
                for (k, v) in b.r:
                    if v > best.get(k, 0):
                        best[k] = v
                b.r = list(best.items())
        for b in writes:
            b.w = tok
            b.r = []

    def op(self, eng, fn, reads=(), writes=()):
        waits = self._waits(eng, self._deps(reads, writes))
        self.cnt[eng] += 1
        tok = (eng, self.cnt[eng])
        self.ops[eng].append((_freeze(fn), waits, (eng, 1)))
        self._upd(tok, reads, writes)

    def dma(self, fn, reads=(), writes=(), q="sp"):
        waits = self._waits(q, self._deps(reads, writes))
        carrier = (list(writes) + list(reads))[0]
        if carrier.sem is None:
            carrier.sem = ("dma", self.n_dma_sems)
            self.n_dma_sems += 1
            self.dma_bufs.append(carrier)
        carrier.semval += 16
        tok = (carrier.sem, carrier.semval)
        self.ops[q].append((_freeze(fn), waits, (carrier.sem, 16)))
        self._upd(tok, reads, writes)

    def barrier(self):
        toks = [(e, self.cnt[e]) for e in ENGS if e != "sp" and self.cnt[e] > 0]
        toks += [(b.sem, b.semval) for b in self.dma_bufs if b.semval > 0]
        for e in ENGS:
            waits = self._waits(e, [t for t in toks if t[0] != e])
            if waits:
                self.ops[e].append((None, waits, None))

    def emit(self, nc, stack):
        sems = {}
        for e in ENGS:
            if e != "sp":
                sems[e] = stack.enter_context(nc.semaphore("s_" + e))
        for i in range(self.n_dma_sems):
            sems[("dma", i)] = stack.enter_context(nc.semaphore("s_dma%d" % i))
        block = stack.enter_context(nc.Block())

        def run(engname):
            def body(engine):
                for fn, waits, inc in self.ops[engname]:
                    for k, v in waits:
                        engine.wait_ge(sems[k], v)
                    if fn is not None:
                        fn(engine).then_inc(sems[inc[0]], inc[1])
            return body

        block.tensor(run("pe"))
        block.scalar(run("act"))
        block.vector(run("dve"))
        block.gpsimd(run("pool"))
        block.sync(run("sp"))


def t5_thresholds():
    n = np.arange(0, 512)
    nf = np.maximum(n, 1).astype(np.float32)
    large = 16 + (np.log(nf / np.float32(16)) / np.float32(math.log(128 / 16)) * np.float32(16)).astype(np.int32)
    large = np.minimum(large, 31)
    b = np.where(n < 16, n, large)
    return [int(np.min(n[b >= k])) for k in range(1, 32)]


def build(S):
    NCH = S // 512
    NSL = NCH // 2
    TO = NSL * 512
    NKB = S // 128
    NT1 = S // 512
    QWK = S // 4
    QWQ = TO // 4
    assert QWQ >= 512
    SCALE = float((64 + 32) ** -0.5)

    nc = bass.Bass("TRN2", target_bir_lowering=False)

    def din(name, shape, dt=F32):
        return nc.dram_tensor(name, list(shape), dt, kind="ExternalInput").ap()

    xT_full = din("xT_full", [D_MODEL, S])
    xT_own = din("xT_own", [D_MODEL, NSL * 640])
    x_own = din("x_own", [TO, D_MODEL])
    pos_full = din("pos_full", [1, S], I32)
    pos_own = din("pos_own", [1, TO], I32)
    pos_bias = din("pos_bias", [1, 256], I32)
    valid_in = din("valid", [128, NSL])
    masks_in = din("masks", [16, 128, 512])
    consts_in = din("consts", [128, 8])
    g1_in = din("g1", [128, 8])
    gq_in = din("gq", [128, 2])
    gk_in = din("gk", [128, 1])
    w1_in = din("w1", [D_MODEL, 320])
    w2_in = din("w2", [D_MODEL, 256])
    w3_in = din("w3", [D_MODEL, 1792])
    wqb_in = din("wqb", [256, 768])
    wqbs_in = din("wqbs", [256, 768])
    wkvb_in = din("wkvb", [128, 1024])
    wo_in = din("wo", [D_MODEL, D_MODEL])
    sinks_in = din("sinks", [1, 8])
    rb_in = din("rb", [1, 256])
    fg_in = din("fg", [1, D_MODEL])
    out_d = nc.dram_tensor("out", [TO, D_MODEL], F32, kind="ExternalOutput").ap()

    P = Prog()
    uid = [0]

    def SB(name, shape, dt, off):
        assert off % 32 == 0, (name, off)
        nb = int(np.prod(shape[1:])) * (2 if dt == BF16 else 4)
        assert SBUF_BASE <= off and off + nb <= SBUF_END, (name, off, nb)
        uid[0] += 1
        return nc.alloc_sbuf_tensor_at("%s_%d" % (name, uid[0]), list(shape), dt, offset=off)

    class Arena:
        def __init__(self, start, end):
            self.p = start
            self.end = end

        def get(self, name, shape, dt):
            nb = int(np.prod(shape[1:])) * (2 if dt == BF16 else 4)
            nb = (nb + 31) // 32 * 32
            t = SB(name, shape, dt, self.p)
            self.p += nb
            assert self.p <= self.end, (name, self.p, self.end)
            return t

    o = SBUF_BASE
    CONST0 = o; o += 20 * 1024
    BIAS0 = o; o += 8 * 1024
    W0 = o; o += 81 * 1024
    KVN0 = o; o += S * 2
    KT0 = o; o += S * 2
    QT0 = o; o += 8 * TO * 2
    QTEND = o
    assert o <= SBUF_END, o
    W1END = KVN0

    with ExitStack() as st:
        PSX = [st.enter_context(nc.psum_tensor("psx%d" % i, [128, 1024], F32)) for i in range(4)]

        def bank(i):
            return PSX[i // 2][:, (i % 2) * 512:(i % 2) * 512 + 512]

        def bank_t(i):
            return PSX[i // 2], (i % 2) * 512

        PB = [Buf("psb%d" % i) for i in range(8)]

        ca = Arena(CONST0, BIAS0)
        CONSTS = ca.get("consts", [128, 8], F32)
        G1 = ca.get("g1", [128, 8], F32)
        GQ = ca.get("gq", [128, 2], F32)
        GK = ca.get("gk", [128, 1], F32)
        EPSC = ca.get("eps", [128, 1], F32)
        PIH = ca.get("pih", [128, 1], F32)
        ESK = ca.get("esk", [128, 8], F32)
        VALID = ca.get("valid", [128, NSL], F32)
        IDENT = ca.get("ident", [128, 128], BF16)
        ONES = ca.get("ones", [128, 128], BF16)
        TQK = ca.get("tqk", [128, 128], F32)
        WKVB = ca.get("wkvb", [128, 1024], BF16)
        WQB = ca.get("wqb", [128, 2, 768], BF16)
        WQBS = ca.get("wqbs", [128, 2, 768], BF16)
        RB = ca.get("rb", [128, 32, 8], F32)
        DIFF = ca.get("diff", [128, 31, 8], F32)
        FG = ca.get("fg", [128, D_MODEL], F32)
        bC = Buf("consts")

        YT = SB("yt", [128, 4, TO], BF16, W0)
        bYT = [Buf("yt%d" % j) for j in range(NSL)]
        BIAS = [SB("bias%d" % t, [128, 8, 128], F32, BIAS0 + t * 4096) for t in range(2)]
        bBIAS = Buf("bias")
        KVN = SB("kvn", [128, S], BF16, KVN0)
        bKVN = Buf("kvn")
        KT = SB("kt", [128, S], BF16, KT0)
        bKTt = [Buf("ktn%d" % t) for t in range(NT1)]
        bKTr = Buf("ktr")
        QT = SB("qt", [128, 8, TO], BF16, QT0)
        bQT = [Buf("qt%d" % h) for h in range(8)]

        wa = Arena(W0, W1END)
        XT0 = wa.get("xt", [128, 8, 512], F32)
        HTs = [wa.get("ht%d" % i, [128, 8, 512], BF16) for i in range(2)]; bHTs = [Buf("ht0"), Buf("ht1")]
        bXTs = [Buf("xt0"), Buf("xt1")]
        RS = wa.get("rs", [128, 512], F32); bRS = Buf("rs")
        LNT = wa.get("lnt", [128, 512], F32); bLNT = Buf("lnt")
        SQ2 = wa.get("sq2", [128, 2, 512], BF16); bSQ2 = Buf("sq2")
        RS2 = wa.get("rs2", [128, 512], F32); bRS2 = Buf("rs2")
        LNT2 = wa.get("lnt2", [128, 512], F32); bLNT2 = Buf("lnt2")
        T1 = wa.get("t1", [128, 512], F32); bT1 = Buf("t1")
        T2 = wa.get("t2", [128, 512], F32); bT2 = Buf("t2")
        QN = wa.get("qn", [128, 2, 512], BF16); bQN = Buf("qn")
        W2 = wa.get("w2", [128, 8, 256], BF16)
        QTAB = [wa.get("qtab%d" % i, [128, QWQ], F32) for i in range(2)]
        stg_off = wa.p
        XT1 = wa.get("xt1", [128, 8, 512], F32)
        XTs = [XT0, XT1]
        STG = [SB("stg%d" % i, [128, 1792], F32, stg_off + i * 7168) for i in range(2)]
        bSTG = [Buf("stg0"), Buf("stg1")]
        bQTAB = Buf("qtab")
        qa = Arena(QT0, SBUF_END)
        W1 = qa.get("w1", [128, 8, 320], BF16)
        KTAB = [qa.get("ktab%d" % i, [128, QWK], F32) for i in range(2)]
        bKTAB = Buf("ktab")
        POSI = qa.get("posi", [128, QWK], I32)
        POSF = qa.get("posf", [128, QWK], F32)
        ANG = qa.get("ang", [128, QWK], F32)
        KI = qa.get("ki", [128, QWK], I32)
        KF = qa.get("kf", [128, QWK], F32)
        bPOSI, bPOSF, bANG, bKI, bKF = Buf("posi"), Buf("posf"), Buf("ang"), Buf("ki"), Buf("kf")
        TI = qa.get("ti", [128, 128], I32)
        bBW = Buf("biaswork")

        P.dma(lambda e: e.dma_start(out=CONSTS[:], in_=consts_in), writes=[bC])
        P.dma(lambda e: e.dma_start(out=G1[:], in_=g1_in), writes=[bC])
        P.dma(lambda e: e.dma_start(out=GQ[:], in_=gq_in), writes=[bC])
        P.dma(lambda e: e.dma_start(out=GK[:], in_=gk_in), writes=[bC])
        P.dma(lambda e: e.dma_start(out=VALID[:], in_=valid_in), writes=[bC])
        P.dma(lambda e: e.dma_start(out=ESK[:], in_=sinks_in.partition_broadcast(128)), writes=[bC])
        P.dma(lambda e: e.dma_start(out=RB[:], in_=rb_in.partition_broadcast(128)), writes=[bC])
        P.dma(lambda e: e.dma_start(out=FG[:], in_=fg_in.partition_broadcast(128)), writes=[bC])
        P.op("pool", lambda e: e.memset(EPSC[:], EPS), writes=[bC])
        P.op("pool", lambda e: e.memset(PIH[:], math.pi / 2), writes=[bC])
        P.op("pool", lambda e: e.memset(ONES[:], 1.0), writes=[bC])
        P.op("pool", lambda e: e.iota(TI[:], pattern=[[1, 128]], base=0, channel_multiplier=-1), writes=[bBW])
        P.op("pool", lambda e: e.tensor_copy(out=TQK[:], in_=TI[:]), reads=[bBW], writes=[bC])
        P.op("pool", lambda e: e.tensor_scalar(out=IDENT[:], in0=TQK[:], scalar1=0.0, scalar2=None, op0=ALU.is_equal),
             reads=[bC], writes=[bC])
        P.op("act", lambda e: e.activation(out=ESK[:], in_=ESK[:], func=AF.Exp), reads=[bC], writes=[bC])

        stg_i = [0]

        def load_w(src_ap, ncols, dst_ap, gain_col, eng="dve"):
            i = stg_i[0] % 2
            stg_i[0] += 1
            sv = STG[i][:, 0:ncols]
            P.dma(lambda e: e.dma_start(out=sv, in_=src_ap), writes=[bSTG[i]])
            P.op(eng, lambda e: e.tensor_scalar(out=dst_ap, in0=sv, scalar1=gain_col, scalar2=None, op0=ALU.mult),
                 reads=[bSTG[i], bC], writes=[bC])

        for kc in range(NKC):
            load_w(w1_in[kc * 128:(kc + 1) * 128, :], 320, W1[:, kc, :], G1[:, kc:kc + 1])
        load_w(wkvb_in, 1024, WKVB[:], GK[:, 0:1])

        def rope_tables(pos_ap, QW, TAB, bTAB):
            for q in range(4):
                P.dma(lambda e, q=q: e.dma_start(out=POSI[32 * q:32 * q + 32, 0:QW],
                                                 in_=pos_ap[:, q * QW:(q + 1) * QW].partition_broadcast(32)),
                      writes=[bPOSI])
            P.op("dve", lambda e: e.tensor_copy(out=POSF[:, 0:QW], in_=POSI[:, 0:QW]), reads=[bPOSI], writes=[bPOSF])
            A_, KI_, KF_ = ANG[:, 0:QW], KI[:, 0:QW], KF[:, 0:QW]
            for ti in range(2):
                if ti == 0:
                    P.op("dve", lambda e: e.tensor_scalar(out=A_, in0=POSF[:, 0:QW], scalar1=CONSTS[:, 0:1], scalar2=PIH[:, 0:1],
                                                          op0=ALU.mult, op1=ALU.add), reads=[bPOSF, bC], writes=[bANG])
                else:
                    P.op("dve", lambda e: e.tensor_scalar(out=A_, in0=POSF[:, 0:QW], scalar1=CONSTS[:, 0:1], scalar2=None,
                                                          op0=ALU.mult), reads=[bPOSF, bC], writes=[bANG])
                P.op("dve", lambda e: e.tensor_scalar(out=KI_, in0=A_, scalar1=1.0 / TWO_PI, scalar2=None, op0=ALU.mult),
                     reads=[bANG], writes=[bKI])
                P.op("dve", lambda e: e.tensor_copy(out=KF_, in_=KI_), reads=[bKI], writes=[bKF])
                P.op("dve", lambda e: e.scalar_tensor_tensor(out=A_, in0=KF_, scalar=-C1, in1=A_, op0=ALU.mult, op1=ALU.add),
                     reads=[bKF, bANG], writes=[bANG])
                P.op("dve", lambda e: e.scalar_tensor_tensor(out=A_, in0=KF_, scalar=-C2, in1=A_, op0=ALU.mult, op1=ALU.add),
                     reads=[bKF, bANG], writes=[bANG])
                P.op("dve", lambda e: e.tensor_scalar(out=KF_, in0=A_, scalar1=math.pi, scalar2=-TWO_PI, op0=ALU.is_gt, op1=ALU.mult),
                     reads=[bANG], writes=[bKF])
                P.op("dve", lambda e: e.tensor_tensor(out=A_, in0=A_, in1=KF_, op=ALU.add), reads=[bANG, bKF], writes=[bANG])
                P.op("dve", lambda e: e.tensor_scalar(out=KF_, in0=A_, scalar1=-math.pi, scalar2=TWO_PI, op0=ALU.is_lt, op1=ALU.mult),
                     reads=[bANG], writes=[bKF])
                P.op("dve", lambda e: e.tensor_tensor(out=A_, in0=A_, in1=KF_, op=ALU.add), reads=[bANG, bKF], writes=[bANG])
                P.op("dve", lambda e: e.tensor_scalar(out=A_, in0=A_, scalar1=3.14159, scalar2=-3.14159, op0=ALU.min, op1=ALU.max),
                     reads=[bANG], writes=[bANG])
                if ti == 0:
                    P.op("act", lambda e: e.activation(out=TAB[0][:, 0:QW], in_=A_, func=AF.Sin), reads=[bANG], writes=[bTAB])
                else:
                    P.op("act", lambda e: e.activation(out=TAB[1][:, 0:QW], in_=A_, func=AF.Sin, scale=CONSTS[:, 1:2]),
                         reads=[bANG, bC], writes=[bTAB])

        for kc in range(NKC):
            load_w(w2_in[kc * 128:(kc + 1) * 128, :], 256, W2[:, kc, :], G1[:, kc:kc + 1])
        for a in range(2):
            load_w(wqb_in[a * 128:(a + 1) * 128, :], 768, WQB[:, a, :], GQ[:, a:a + 1])
            load_w(wqbs_in[a * 128:(a + 1) * 128, :], 768, WQBS[:, a, :], GQ[:, a:a + 1])
        rope_tables(pos_full, QWK, KTAB, bKTAB)
        P.barrier()
        if STOP_AFTER == 'p0':
            P.emit(nc, st)
            return nc
        rope_tables(pos_own, QWQ, QTAB, bQTAB)

        def norm_tile(src_ap, XTt, bXTt, HTt, bHTt, RSt, LNTt, ntok, ss_psx, bss):
            P.dma(lambda e: e.dma_start(out=XTt[:, :, 0:ntok], in_=src_ap.rearrange("(kc p) t -> p kc t", p=128)), writes=[bXTt])
            P.op("act", lambda e: e.activation(out=HTt[:, :, 0:ntok], in_=XTt[:, :, 0:ntok], func=AF.Square), reads=[bXTt], writes=[bHTt])
            for (a, b_) in ((0, min(ntok, 512)), (512, ntok)):
                if b_ <= a:
                    continue
                for kc in range(NKC):
                    P.op("pe", lambda e, kc=kc, a=a, b_=b_: e.matmul(ss_psx[:, a:b_], lhsT=ONES[:], rhs=HTt[:, kc, a:b_],
                                                                   start=(kc == 0), stop=(kc == NKC - 1)),
                         reads=[bHTt, bC], writes=bss)
            P.op("act", lambda e: e.activation(out=LNTt[:, 0:ntok], in_=ss_psx[:, 0:ntok], func=AF.Ln, bias=EPSC[:, 0:1], scale=1.0 / D_MODEL),
                 reads=bss + [bC], writes=[bLNT])
            P.op("act", lambda e: e.activation(out=RSt[:, 0:ntok], in_=LNTt[:, 0:ntok], func=AF.Exp, scale=-0.5), reads=[bLNT], writes=[bRS])
            P.op("dve", lambda e: e.tensor_tensor(out=HTt[:, :, 0:ntok], in0=XTt[:, :, 0:ntok],
                                                  in1=RSt[:, 0:ntok].unsqueeze(1).to_broadcast([128, 8, ntok]), op=ALU.mult),
                 reads=[bXTt, bRS], writes=[bHTt])

        def p1_norm(t):
            norm_tile(xT_full[:, t * 512:t * 512 + 512], XTs[t % 2], bXTs[t % 2], HTs[t % 2], bHTs[t % 2], RS, LNT, 512, PSX[0], [PB[0]])

        def p1_main(t):
            c0 = t * 512
            HT = HTs[t % 2]
            bHT = bHTs[t % 2]
            for (bk, lo, hi, M) in ((2, 0, 128, 128), (3, 128, 224, 96), (4, 224, 320, 96)):
                for kc in range(NKC):
                    P.op("pe", lambda e, kc=kc, bk=bk, lo=lo, hi=hi, M=M: e.matmul(bank(bk)[0:M, :], lhsT=W1[:, kc, lo:hi], rhs=HT[:, kc, :],
                                                                                start=(kc == 0), stop=(kc == NKC - 1)),
                         reads=[bHT, bC], writes=[PB[bk]])
            P.op("act", lambda e: e.activation(out=SQ2[:, 0, :], in_=bank(2), func=AF.Square), reads=[PB[2]], writes=[bSQ2])
            P.op("pe", lambda e: e.matmul(bank(5), lhsT=ONES[:], rhs=SQ2[:, 0, :], start=True, stop=True), reads=[bSQ2, bC], writes=[PB[5]])
            P.op("act", lambda e: e.activation(out=LNT2[:], in_=bank(5), func=AF.Ln, bias=EPSC[:, 0:1], scale=1.0 / 128), reads=[PB[5], bC], writes=[bLNT2])
            P.op("act", lambda e: e.activation(out=RS2[:], in_=LNT2[:], func=AF.Exp, scale=-0.5), reads=[bLNT2], writes=[bRS2])
            P.op("dve", lambda e, c0=c0: e.tensor_tensor(out=KVN[:, c0:c0 + 512], in0=bank(2), in1=RS2[:], op=ALU.mult),
                 reads=[PB[2], bRS2], writes=[bKVN])
            q = c0 // QWK
            off = c0 % QWK
            P.op("dve", lambda e, q=q, off=off: e.tensor_tensor(out=T1[64:96, :], in0=bank(3)[64:96, :], in1=KTAB[0][32 * q:32 * q + 32, off:off + 512], op=ALU.mult),
                 reads=[PB[3], bKTAB], writes=[bT1])
            P.op("dve", lambda e, q=q, off=off: e.tensor_tensor(out=T2[64:96, :], in0=bank(4)[64:96, :], in1=KTAB[1][32 * q:32 * q + 32, off:off + 512], op=ALU.mult),
                 reads=[PB[4], bKTAB], writes=[bT2])
            P.op("dve", lambda e, c0=c0: e.tensor_tensor(out=KT[64:96, c0:c0 + 512], in0=T1[64:96, :], in1=T2[64:96, :], op=ALU.add),
                 reads=[bT1, bT2], writes=[bKTr])

        p1_norm(0)
        for t in range(NT1):
            if t + 1 < NT1:
                p1_norm(t + 1)
            p1_main(t)

        P.barrier()
        if STOP_AFTER == 'p1':
            P.emit(nc, st)
            return nc

        def p2_norm(t):
            norm_tile(xT_own[:, t * 640 + 128:t * 640 + 640], XTs[t % 2], bXTs[t % 2], HTs[t % 2], bHTs[t % 2], RS, LNT, 512, PSX[0], [PB[0]])

        def p2_main(t):
            c0 = t * 512
            HT = HTs[t % 2]
            bHT = bHTs[t % 2]
            for m in range(2):
                for kc in range(NKC):
                    P.op("pe", lambda e, kc=kc, m=m: e.matmul(bank(2 + m), lhsT=W2[:, kc, m * 128:(m + 1) * 128], rhs=HT[:, kc, :],
                                                              start=(kc == 0), stop=(kc == NKC - 1)),
                         reads=[bHT, bC], writes=[PB[2 + m]])
            for m in range(2):
                P.op("act", lambda e, m=m: e.activation(out=SQ2[:, m, :], in_=bank(2 + m), func=AF.Square), reads=[PB[2 + m]], writes=[bSQ2])
            for m in range(2):
                P.op("pe", lambda e, m=m: e.matmul(bank(5), lhsT=ONES[:], rhs=SQ2[:, m, :], start=(m == 0), stop=(m == 1)), reads=[bSQ2, bC], writes=[PB[5]])
            P.op("act", lambda e: e.activation(out=LNT2[:], in_=bank(5), func=AF.Ln, bias=EPSC[:, 0:1], scale=1.0 / 256), reads=[PB[5], bC], writes=[bLNT2])
            P.op("act", lambda e: e.activation(out=RS2[:], in_=LNT2[:], func=AF.Exp, scale=-0.5), reads=[bLNT2], writes=[bRS2])
            for m in range(2):
                P.op("dve", lambda e, m=m: e.tensor_tensor(out=QN[:, m, :], in0=bank(2 + m), in1=RS2[:], op=ALU.mult),
                     reads=[PB[2 + m], bRS2], writes=[bQN])
            q = c0 // QWQ
            off = c0 % QWQ
            for h in range(8):
                bq = 6 + (h % 2)
                bs = 4 if h % 2 == 0 else 1
                for m in range(2):
                    P.op("pe", lambda e, m=m, h=h, bq=bq: e.matmul(bank(bq)[0:96, :], lhsT=WQB[:, m, h * 96:(h + 1) * 96], rhs=QN[:, m, :],
                                                                   start=(m == 0), stop=(m == 1)), reads=[bQN, bC], writes=[PB[bq]])
                for m in range(2):
                    P.op("pe", lambda e, m=m, h=h, bs=bs: e.matmul(bank(bs)[0:96, :], lhsT=WQBS[:, m, h * 96:(h + 1) * 96], rhs=QN[:, m, :],
                                                                   start=(m == 0), stop=(m == 1)), reads=[bQN, bC], writes=[PB[bs]])
                P.op("act", lambda e, h=h, bq=bq, c0=c0: e.activation(out=QT[0:64, h, c0:c0 + 512], in_=bank(bq)[0:64, :], func=AF.Copy, scale=SCALE),
                     reads=[PB[bq]], writes=[bQT[h]])
                P.op("dve", lambda e, bq=bq, q=q, off=off: e.scalar_tensor_tensor(out=T1[64:96, :], in0=bank(bq)[64:96, :], scalar=SCALE,
                                                                                in1=QTAB[0][32 * q:32 * q + 32, off:off + 512], op0=ALU.mult, op1=ALU.mult),
                     reads=[PB[bq], bQTAB], writes=[bT1])
                P.op("dve", lambda e, bs=bs, q=q, off=off: e.scalar_tensor_tensor(out=T2[64:96, :], in0=bank(bs)[64:96, :], scalar=SCALE,
                                                                                in1=QTAB[1][32 * q:32 * q + 32, off:off + 512], op0=ALU.mult, op1=ALU.mult),
                     reads=[PB[bs], bQTAB], writes=[bT2])
                P.op("dve", lambda e, h=h, c0=c0: e.tensor_tensor(out=QT[64:96, h, c0:c0 + 512], in0=T1[64:96, :], in1=T2[64:96, :], op=ALU.add),
                     reads=[bT1, bT2], writes=[bQT[h]])

        p2_norm(0)
        for t in range(NSL):
            if t + 1 < NSL:
                p2_norm(t + 1)
            p2_main(t)

        P.barrier()
        if STOP_AFTER == 'p2':
            P.emit(nc, st)
            return nc

        pa = Arena(W0 + 4 * TO * 2, W1END)
        VP = pa.get("vp", [128, NKB, 128], BF16); bVPg = [Buf("vp%d" % g) for g in range(NKB // 8)]
        MASKT = pa.get("maskt", [128, 16, 512], BF16); bMASK = Buf("mask")
        PT = [pa.get("pt%d" % i, [128, 512], BF16) for i in range(4)]
        bPT = [Buf("pt%d" % i) for i in range(4)]
        RC = pa.get("rc", [128, 512], F32); bRC = Buf("rc")
        tmpb_off = pa.p
        TMPB = pa.get("tmpb", [128, 8, 128], F32)
        MSTG = [SB("mstg%d" % i, [128, 512], F32, tmpb_off + i * 2048) for i in range(2)]
        bMSTG = [Buf("mstg0"), Buf("mstg1")]
        for i in range(16):
            k = i % 2
            P.dma(lambda e, i=i, k=k: e.dma_start(out=MSTG[k][:], in_=masks_in[i]), writes=[bMSTG[k]])
            P.op("dve", lambda e, i=i, k=k: e.tensor_copy(out=MASKT[:, i, :], in_=MSTG[k][:]), reads=[bMSTG[k]], writes=[bMASK])

        PB256 = pa.get("pb256", [128, 256], I32)
        PQF = pa.get("pqf", [128, 128], F32)
        KCOL = pa.get("kcol", [128, 2], I32)
        KCOLF = pa.get("kcolf", [128, 2], F32)
        DT_ = [pa.get("dt%d" % t, [128, 128], F32) for t in range(2)]
        IND = pa.get("ind", [128, 128], F32)
        MSK = pa.get("msk", [128, 128], F32)
        thr = t5_thresholds()
        P.dma(lambda e: e.dma_start(out=PB256[:], in_=pos_bias.partition_broadcast(128)), writes=[bBW])
        P.dma(lambda e: e.dma_start(out=KCOL[:, 0:1], in_=pos_bias[0:1, 0:128].rearrange("o (p f) -> p (o f)", f=1)), writes=[bBW])
        P.dma(lambda e: e.dma_start(out=KCOL[:, 1:2], in_=pos_bias[0:1, 128:256].rearrange("o (p f) -> p (o f)", f=1)), writes=[bBW])
        P.op("pool", lambda e: e.tensor_copy(out=PQF[:], in_=PB256[:, 128:256]), reads=[bBW], writes=[bBW])
        P.op("pool", lambda e: e.tensor_copy(out=KCOLF[:], in_=KCOL[:]), reads=[bBW], writes=[bBW])
        P.op("pool", lambda e: e.tensor_tensor(out=DIFF[:], in0=RB[:, 1:32, :], in1=RB[:, 0:31, :], op=ALU.subtract), reads=[bC, bBW], writes=[bBW])
        for t in range(2):
            P.op("pool", lambda e, t=t: e.tensor_scalar(out=DT_[t][:], in0=PQF[:], scalar1=KCOLF[:, t:t + 1], scalar2=None, op0=ALU.subtract),
                 reads=[bBW], writes=[bBW])
            P.op("pool", lambda e, t=t: e.tensor_copy(out=BIAS[t][:], in_=RB[:, 0, :].unsqueeze(2).to_broadcast([128, 8, 128])),
                 reads=[bC], writes=[bBIAS])
            for b in range(31):
                P.op("pool", lambda e, t=t, b=b: e.tensor_scalar(out=IND[:], in0=DT_[t][:], scalar1=float(thr[b]), scalar2=None, op0=ALU.is_ge),
                     reads=[bBW], writes=[bBW])
                P.op("pool", lambda e, b=b: e.tensor_tensor(out=TMPB[:], in0=IND[:].unsqueeze(1).to_broadcast([128, 8, 128]),
                                                            in1=DIFF[:, b, :].unsqueeze(2).to_broadcast([128, 8, 128]), op=ALU.mult),
                     reads=[bBW], writes=[bBW, bMSTG[0], bMSTG[1]])
                P.op("pool", lambda e, t=t: e.tensor_tensor(out=BIAS[t][:], in0=BIAS[t][:], in1=TMPB[:], op=ALU.add),
                     reads=[bBW, bBIAS], writes=[bBIAS])
            if t == 0:
                P.op("pool", lambda e: e.tensor_scalar(out=MSK[:], in0=TQK[:], scalar1=0.0, scalar2=NEG, op0=ALU.is_ge, op1=ALU.mult),
                     reads=[bC], writes=[bBW])
            else:
                P.op("pool", lambda e: e.tensor_scalar(out=MSK[:], in0=TQK[:], scalar1=0.0, scalar2=NEG, op0=ALU.is_lt, op1=ALU.mult),
                     reads=[bC], writes=[bBW])
            P.op("pool", lambda e, t=t: e.tensor_tensor(out=BIAS[t][:], in0=BIAS[t][:], in1=MSK[:].unsqueeze(1).to_broadcast([128, 8, 128]), op=ALU.add),
                 reads=[bBW, bBIAS], writes=[bBIAS])

        steps = [(h, j, kb) for h in range(8) for j in range(NSL) for kb in range(8 * j + 8)]
        LA = 3
        NR = 4

        def build_kv(h, jn):
            vo = 64 * (h % 2)
            so = 64 - vo
            for t in (2 * jn, 2 * jn + 1):
                bk = 6 + (t % 2)
                P.op("pe", lambda e, t=t, bk=bk: e.matmul(bank(bk)[0:64, :], lhsT=WKVB[:, h * 128:h * 128 + 64], rhs=KVN[:, t * 512:(t + 1) * 512],
                                                          start=True, stop=True), reads=[bKVN, bC], writes=[PB[bk]])
                P.op("dve", lambda e, t=t, bk=bk: e.tensor_copy(out=KT[0:64, t * 512:(t + 1) * 512], in_=bank(bk)[0:64, :]), reads=[PB[bk]], writes=[bKTt[t]])
            g8 = jn
            P.op("dve", lambda e: e.memset(VP[:, g8 * 8:(g8 + 1) * 8, so:so + 64], 1.0), writes=[bVPg[g8]])
            for bb in range(8):
                blk = g8 * 8 + bb
                P.op("pe", lambda e, blk=blk, bb=bb: e.matmul(bank(7)[:, bb * 64:(bb + 1) * 64], lhsT=KVN[:, blk * 128:(blk + 1) * 128],
                                                              rhs=WKVB[:, h * 128 + 64:h * 128 + 128], start=True, stop=True),
                     reads=[bKVN, bC], writes=[PB[7]])
            P.op("dve", lambda e: e.tensor_copy(out=VP[:, g8 * 8:(g8 + 1) * 8, vo:vo + 64], in_=bank(7).rearrange("p (a b) -> p a b", a=8)),
                 reads=[PB[7]], writes=[bVPg[g8]])

        def emit_qk(idx):
            h, j, kb = steps[idx]
            sb = idx % NR
            P.op("pe", lambda e: e.matmul(bank(sb), lhsT=KT[0:96, kb * 128:(kb + 1) * 128],
                                          rhs=QT[0:96, h, j * 512:(j + 1) * 512], start=True, stop=True),
                 reads=[bKTt[kb // 4], bKTr, bQT[h]], writes=[PB[sb]])

        build_kv(0, 0)
        for idx in range(min(LA, len(steps))):
            emit_qk(idx)
        XH = 16 if NSL >= 2 else 4
        for idx, (h, j, kb) in enumerate(steps):
            vo = 64 * (h % 2)
            so = 64 - vo
            if j + 1 < NSL and kb == max(0, 8 * j + 8 - 12):
                build_kv(h, j + 1)
            if j == NSL - 1 and kb == XH and h + 1 < 8:
                build_kv(h + 1, 0)
            if idx + LA < len(steps):
                emit_qk(idx + LA)
            sb = idx % NR
            ob = 4 + ((h * NSL + j) % 2)
            last = 8 * j + 7
            P.op("act", lambda e: e.activation(out=PT[sb][:], in_=bank(sb), func=AF.Exp), reads=[PB[sb]], writes=[bPT[sb]])
            if kb >= 8 * j:
                mi = (j % 2) * 8 + (kb - 8 * j)
                P.op("dve", lambda e: e.tensor_tensor(out=PT[sb][:], in0=PT[sb][:], in1=MASKT[:, mi, :], op=ALU.mult),
                     reads=[bPT[sb], bMASK], writes=[bPT[sb]])
            P.op("pe", lambda e: e.matmul(bank(ob), lhsT=VP[:, kb, :], rhs=PT[sb][:], start=(kb == 0), stop=(kb == last)),
                 reads=[bVPg[kb // 8], bPT[sb]], writes=[PB[ob]])
            if kb == last:
                P.op("dve", lambda e: e.reciprocal(out=RC[so:so + 64, :], in_=bank(ob)[so:so + 64, :]), reads=[PB[ob]], writes=[bRC])
                P.op("dve", lambda e: e.tensor_tensor(out=YT[vo:vo + 64, h // 2, j * 512:(j + 1) * 512],
                                                      in0=bank(ob)[vo:vo + 64, :], in1=RC[so:so + 64, :], op=ALU.mult),
                     reads=[PB[ob], bRC], writes=[bYT[j]])

        P.barrier()
        if STOP_AFTER == 'p2b':
            P.emit(nc, st)
            return nc

        p3 = Arena(W0 + 4 * TO * 2, SBUF_END)
        W3 = p3.get("w3", [128, 8, 1792], BF16)
        WO = p3.get("wo", [128, 8, 1024], BF16)
        xt3_off = p3.p
        XT3 = p3.get("xt3", [128, 8, 640], F32)
        HT3s = [p3.get("ht3_%d" % i, [128, 8, 640], BF16) for i in range(2)]; bHT3s = [Buf("ht3_0"), Buf("ht3_1")]
        gt1_off = p3.p
        GT1 = [p3.get("gt1_%d" % i, [128, 512], F32) for i in range(2)]; bGT1 = [Buf("gt1_0"), Buf("gt1_1")]
        RS3 = p3.get("rs3", [128, 640], F32)
        LNT3 = RS3
        bXT3 = Buf("xt3")
        QS = p3.get("qs", [128, 4, 512], BF16); bQS = Buf("qs")
        KS = p3.get("ks", [128, 640], BF16); bKS = Buf("ks")
        VS = p3.get("vs", [128, 5, 2, 128], BF16); bVS = Buf("vs")
        PTS = [[p3.get("pts%d_%d" % (p_, i), [128, 512], BF16) for i in range(2)] for p_ in range(2)]
        bPTS = [[Buf("pts%d_%d" % (p_, i)) for i in range(2)] for p_ in range(2)]
        RCS = [p3.get("rcs%d" % p_, [128, 512], F32) for p_ in range(2)]; bRCS = [Buf("rcs0"), Buf("rcs1")]
        YS = p3.get("ys", [128, 4, 512], BF16); bYS = Buf("ys")
        SGs = [p3.get("sg%d" % i, [128, 8, 512], BF16) for i in range(2)]; bSGs = [Buf("sg0"), Buf("sg1")]
        XO = [p3.get("xo%d" % i, [128, 1024], F32) for i in range(2)]; bXO = [Buf("xo0"), Buf("xo1")]
        RR = p3.get("rr", [128, 1024], F32); bRR = Buf("rr")
        JNK = SB("jnk", [128, 1024], BF16, gt1_off); bJNK = bGT1[0]
        ONEC = p3.get("onec", [128, 1], F32)
        OB = [p3.get("ob%d" % i, [128, 1024], F32) for i in range(2)]; bOB = [Buf("ob0"), Buf("ob1")]
        SS4 = p3.get("ss4", [128, 4], F32); bSS4 = Buf("ss4")
        STG3 = [SB("stg3_%d" % i, [128, 1792], F32, xt3_off + i * 7168) for i in range(2)]; bSTG3 = [Buf("stg3_0"), Buf("stg3_1")]
        BHLT = SB("bhlt", [128, 8, 512], BF16, xt3_off + 14336)
        BDT = SB("bdt", [128, 512], F32, xt3_off + 14336 + 8192)
        BHL = SB("bhl", [128, 8, 512], BF16, BIAS0)
        bBHL = Buf("bhl")
        bW3 = Buf("w3")

        for kc in range(NKC):
            i = kc % 2
            P.dma(lambda e, kc=kc, i=i: e.dma_start(out=STG3[i][:], in_=w3_in[kc * 128:(kc + 1) * 128, :]), writes=[bSTG3[i]])
            if kc % 2 == 0:
                P.op("dve", lambda e, kc=kc, i=i: e.tensor_scalar(out=W3[:, kc, :], in0=STG3[i][:], scalar1=G1[:, kc:kc + 1], scalar2=None, op0=ALU.mult),
                     reads=[bSTG3[i], bC], writes=[bW3])
            else:
                P.op("act", lambda e, kc=kc, i=i: e.activation(out=W3[:, kc, :], in_=STG3[i][:], func=AF.Copy, scale=G1[:, kc:kc + 1]),
                     reads=[bSTG3[i], bC], writes=[bW3])
        for kc in range(NKC):
            i = kc % 2
            P.dma(lambda e, kc=kc, i=i: e.dma_start(out=STG3[i][:, 0:1024], in_=wo_in[kc * 128:(kc + 1) * 128, :]), writes=[bSTG3[i]])
            if kc % 2 == 0:
                P.op("dve", lambda e, kc=kc, i=i: e.tensor_copy(out=WO[:, kc, :], in_=STG3[i][:, 0:1024]), reads=[bSTG3[i]], writes=[bW3])
            else:
                P.op("act", lambda e, kc=kc, i=i: e.activation(out=WO[:, kc, :], in_=STG3[i][:, 0:1024], func=AF.Copy), reads=[bSTG3[i]], writes=[bW3])
        P.op("pool", lambda e: e.memset(VS[:, :, :, 64:128], 1.0), writes=[bVS])
        bONE = Buf("onec")
        P.op("pool", lambda e: e.memset(ONEC[:], 1.0), writes=[bONE])
        for t in range(2):
            for g in range(2):
                k = t * 2 + g
                src = BIAS[t][:, 4 * g:4 * g + 4, :].rearrange("p a b -> p (a b)")
                P.op("pool", lambda e, k=k, src=src: e.tensor_copy(out=BHLT[:, k, :], in_=src), reads=[bBIAS], writes=[bBHL])
                P.op("pool", lambda e, k=k, src=src: e.tensor_tensor(out=BDT[:], in0=src, in1=BHLT[:, k, :], op=ALU.subtract), reads=[bBIAS, bBHL], writes=[bBHL])
                P.op("pool", lambda e, k=k: e.tensor_copy(out=BHLT[:, 4 + k, :], in_=BDT[:]), reads=[bBHL], writes=[bBHL])
        P.barrier()
        P.op("pool", lambda e: e.tensor_copy(out=BHL[:], in_=BHLT[:]), reads=[bBHL], writes=[bBHL])
        P.barrier()
        if STOP_AFTER == 'p3w':
            P.emit(nc, st)
            return nc

        def proj(HT3, bHT3, col0, ncol, bk, ntok=512, tok0=128):
            for kc in range(NKC):
                P.op("pe", lambda e, kc=kc: e.matmul(bank(bk)[0:ncol, 0:ntok], lhsT=W3[:, kc, col0:col0 + ncol], rhs=HT3[:, kc, tok0:tok0 + ntok],
                                                     start=(kc == 0), stop=(kc == NKC - 1)), reads=[bHT3, bW3], writes=[PB[bk]])

        def p3_pre(j):
            norm_tile(xT_own[:, j * 640:(j + 1) * 640], XT3, bXT3, HT3s[j % 2], bHT3s[j % 2], RS3, LNT3, 640, PSX[0], [PB[0], PB[1]])

        def p3_proj(j):
            HT3 = HT3s[j % 2]
            bHT3 = bHT3s[j % 2]
            for u in range(4):
                bk = 2 + (u % 2)
                proj(HT3, bHT3, u * 128, 128, bk)
                P.op("act", lambda e, u=u, bk=bk: e.activation(out=QS[:, u, :], in_=bank(bk), func=AF.Copy, scale=0.125), reads=[PB[bk]], writes=[bQS])
            proj(HT3, bHT3, 512, 128, 4, ntok=512, tok0=128)
            P.op("dve", lambda e: e.tensor_copy(out=KS[:, 128:640], in_=bank(4)), reads=[PB[4]], writes=[bKS])
            proj(HT3, bHT3, 512, 128, 5, ntok=128, tok0=0)
            P.op("dve", lambda e: e.tensor_copy(out=KS[:, 0:128], in_=bank(5)[:, 0:128]), reads=[PB[5]], writes=[bKS])
            for w in range(5):
                for kc in range(NKC):
                    P.op("pe", lambda e, kc=kc, w=w: e.matmul(bank(6)[:, w * 128:(w + 1) * 128] if w < 4 else bank(7)[:, 0:128],
                                                              lhsT=HT3[:, kc, w * 128:(w + 1) * 128], rhs=W3[:, kc, 640:768],
                                                              start=(kc == 0), stop=(kc == NKC - 1)), reads=[bHT3, bW3], writes=[PB[6] if w < 4 else PB[7]])
            P.op("dve", lambda e: e.tensor_copy(out=VS[:, 0:4, :, 0:64], in_=bank(6).rearrange("p (w g d) -> p w g d", w=4, g=2)),
                 reads=[PB[6]], writes=[bVS])
            P.op("dve", lambda e: e.tensor_copy(out=VS[:, 4, :, 0:64], in_=bank(7)[:, 0:128].rearrange("p (g d) -> p g d", g=2)),
                 reads=[PB[7]], writes=[bVS])
            P.op("dve", lambda e: e.tensor_scalar(out=VS[:, 0, :, 64:128], in0=ONES[:].rearrange("p (g d) -> p g d", g=2),
                                                  scalar1=VALID[:, j:j + 1], scalar2=None, op0=ALU.mult), reads=[bC], writes=[bVS])

        def gate_tile(j, gt):
            HT3 = HT3s[j % 2]
            bHT3 = bHT3s[j % 2]
            SG = SGs[j % 2]
            bSG = bSGs[j % 2]
            bk = gt % 2
            gp = gt % 2
            proj(HT3, bHT3, 768 + gt * 128, 128, bk)
            P.op("act", lambda e: e.activation(out=GT1[gp][:], in_=bank(bk), func=AF.Exp, scale=-1.0), reads=[PB[bk]], writes=[bGT1[gp]])
            P.op("act", lambda e: e.activation(out=GT1[gp][:], in_=GT1[gp][:], func=AF.Ln, bias=ONEC[:, 0:1]), reads=[bGT1[gp], bONE], writes=[bGT1[gp]])
            P.op("act", lambda e: e.activation(out=GT1[gp][:], in_=GT1[gp][:], func=AF.Exp, scale=-1.0), reads=[bGT1[gp]], writes=[bGT1[gp]])
            P.op("dve", lambda e: e.tensor_tensor(out=SG[:, gt, :], in0=bank(bk), in1=GT1[gp][:], op=ALU.mult), reads=[PB[bk], bGT1[gp]], writes=[bSG])

        def swa_a(j, n_it):
            i, g = n_it // 2, n_it % 2
            pn = n_it % 2
            ob = 6 + pn
            for t in range(2):
                w = i + t
                sbk = 2 + 2 * pn + t
                k = t * 2 + g
                P.op("pe", lambda e, w=w, sbk=sbk: e.matmul(bank(sbk), lhsT=KS[64 * g:64 * g + 64, w * 128:(w + 1) * 128],
                                                          rhs=QS[64 * g:64 * g + 64, :, i * 128:(i + 1) * 128], start=True, stop=False),
                     reads=[bKS, bQS], writes=[PB[sbk]])
                P.op("pe", lambda e, sbk=sbk, k=k: e.matmul(bank(sbk), lhsT=IDENT[:], rhs=BHL[:, k, :], start=False, stop=False),
                     reads=[bBHL, bC], writes=[PB[sbk]])
                P.op("pe", lambda e, sbk=sbk, k=k: e.matmul(bank(sbk), lhsT=IDENT[:], rhs=BHL[:, 4 + k, :], start=False, stop=True),
                     reads=[bBHL, bC], writes=[PB[sbk]])
                P.op("act", lambda e, t=t, sbk=sbk: e.activation(out=PTS[pn][t][:], in_=bank(sbk), func=AF.Exp), reads=[PB[sbk]], writes=[bPTS[pn][t]])

        def swa_b(j, n_it):
            i, g = n_it // 2, n_it % 2
            pn = n_it % 2
            ob = 6 + pn
            for t in range(2):
                w = i + t
                P.op("pe", lambda e, t=t, w=w: e.matmul(bank(ob), lhsT=VS[:, w, g, :], rhs=PTS[pn][t][:], start=(t == 0), stop=(t == 1)),
                     reads=[bVS, bPTS[pn][t]], writes=[PB[ob]])
            P.op("dve", lambda e: e.tensor_tensor(out=RCS[pn][64:128, :].rearrange("p (a b) -> p a b", a=4),
                                                  in0=bank(ob)[64:128, :].rearrange("p (a b) -> p a b", a=4),
                                                  in1=ESK[64:128, 4 * g:4 * g + 4].unsqueeze(2).to_broadcast([64, 4, 128]), op=ALU.add),
                 reads=[PB[ob], bC], writes=[bRCS[pn]])
            P.op("act", lambda e: e.activation(out=RCS[pn][64:128, :], in_=RCS[pn][64:128, :], func=AF.Ln), reads=[bRCS[pn]], writes=[bRCS[pn]])
            P.op("act", lambda e: e.activation(out=RCS[pn][64:128, :], in_=RCS[pn][64:128, :], func=AF.Exp, scale=-1.0), reads=[bRCS[pn]], writes=[bRCS[pn]])
            for u in range(4):
                po = 64 * (u % 2)
                tl = 2 * g + u // 2
                P.op("dve", lambda e, u=u, po=po, tl=tl: e.tensor_tensor(out=YS[po:po + 64, tl, i * 128:(i + 1) * 128],
                                                                         in0=bank(ob)[0:64, u * 128:(u + 1) * 128],
                                                                         in1=RCS[pn][64:128, u * 128:(u + 1) * 128], op=ALU.mult),
                     reads=[PB[ob], bRCS[pn]], writes=[bYS])

        def yg(j):
            SG = SGs[j % 2]
            bSG = bSGs[j % 2]
            P.op("dve", lambda e: e.tensor_tensor(out=SG[:, 0:4, :], in0=YT[:, :, j * 512:(j + 1) * 512], in1=SG[:, 0:4, :], op=ALU.mult),
                 reads=[bYT[j], bSG], writes=[bSG])
            P.op("dve", lambda e: e.tensor_tensor(out=SG[:, 4:8, :], in0=YS[:], in1=SG[:, 4:8, :], op=ALU.mult),
                 reads=[bYS, bSG], writes=[bSG])

        def out_block(j, i):
            SG = SGs[j % 2]
            bSG = bSGs[j % 2]
            row0 = j * 512 + i * 128
            k = i % 2
            P.dma(lambda e: e.dma_start(out=XO[k][:], in_=x_own[row0:row0 + 128, :]), writes=[bXO[k]])
            for hf in range(2):
                bk = hf
                for kc in range(NKC):
                    P.op("pe", lambda e, kc=kc, hf=hf, bk=bk: e.matmul(bank(bk), lhsT=SG[:, kc, i * 128:(i + 1) * 128], rhs=WO[:, kc, hf * 512:(hf + 1) * 512],
                                                                       start=(kc == 0), stop=(kc == NKC - 1)), reads=[bSG, bW3], writes=[PB[bk]])
                P.op("dve", lambda e, hf=hf, bk=bk: e.tensor_tensor(out=RR[:, hf * 512:(hf + 1) * 512], in0=bank(bk), in1=XO[k][:, hf * 512:(hf + 1) * 512], op=ALU.add),
                     reads=[PB[bk], bXO[k]], writes=[bRR])
            P.op("act", lambda e: e.activation(out=JNK[:], in_=RR[:], func=AF.Square, accum_out=SS4[:, 0:1]), reads=[bRR], writes=[bJNK, bSS4])
            P.op("act", lambda e: e.activation(out=SS4[:, 1:2], in_=SS4[:, 0:1], func=AF.Ln, bias=EPSC[:, 0:1], scale=1.0 / D_MODEL), reads=[bSS4, bC], writes=[bSS4])
            P.op("act", lambda e: e.activation(out=SS4[:, 2:3], in_=SS4[:, 1:2], func=AF.Exp, scale=-0.5), reads=[bSS4], writes=[bSS4])
            P.op("dve", lambda e: e.scalar_tensor_tensor(out=OB[k][:], in0=RR[:], scalar=SS4[:, 2:3], in1=FG[:], op0=ALU.mult, op1=ALU.mult),
                 reads=[bRR, bSS4, bC], writes=[bOB[k]])
            P.dma(lambda e: e.dma_start(out=out_d[row0:row0 + 128, :], in_=OB[k][:]), reads=[bOB[k]])

        p3_pre(0)
        p3_proj(0)
        for j in range(NSL + 1):
            if j + 1 < NSL:
                p3_pre(j + 1)
            for n_it in range(8):
                if j >= 1 and n_it % 2 == 0:
                    out_block(j - 1, n_it // 2)
                if j < NSL:
                    gate_tile(j, n_it)
                    if n_it == 0:
                        swa_a(j, 0)
                    if n_it + 1 < 8:
                        swa_a(j, n_it + 1)
                    swa_b(j, n_it)
            if j < NSL:
                yg(j)
            if j + 1 < NSL:
                p3_proj(j + 1)

        P.barrier()
        P.emit(nc, st)
    return nc


def chunk_of(j, p):
    return 2 * j + ((j + p) % 2)


def make_masks(p):
    m = np.zeros((2, 8, 128, 512), np.float32)
    k = np.arange(128)[:, None]
    q = np.arange(512)[None, :]
    for s in range(2):
        delta = (s + p) % 2
        for r in range(8):
            vis = (r * 128 + k) <= (delta * 512 + q)
            m[s, r] = np.where(vis, 1.0, 0.0)
    return m.reshape(16, 128, 512)


def prep_inputs(inputs, B, S):
    NSL = S // 1024
    x = np.asarray(inputs["x"], np.float32)
    pos = np.asarray(inputs["positions"], np.int32)
    w_in = np.asarray(inputs["w_in"], np.float32)[0]
    w1 = np.ascontiguousarray(np.concatenate([w_in[:, 256:384], w_in[:, 320:416], w_in[:, 320:384], w_in[:, 400:416], w_in[:, 384:400]], axis=1))
    w2 = np.ascontiguousarray(w_in[:, 0:256])
    qs_cols = []
    for u in range(4):
        qs_cols.append(w_in[:, 416 + u * 64:416 + (u + 1) * 64])
        qs_cols.append(w_in[:, 416 + (4 + u) * 64:416 + (5 + u) * 64])
    w3 = np.ascontiguousarray(np.concatenate(qs_cols + [w_in[:, 928:1056], w_in[:, 1056:1184], w_in[:, 1184:2208]], axis=1))
    wqb = np.asarray(inputs["w_q_b"], np.float32)[0]
    wqbs = wqb.copy()
    for h in range(8):
        b0 = h * 96 + 64
        wqbs[:, b0:b0 + 16] = wqb[:, b0 + 16:b0 + 32]
        wqbs[:, b0 + 16:b0 + 32] = wqb[:, b0:b0 + 16]
    wkvb = np.ascontiguousarray(np.asarray(inputs["w_kv_b"], np.float32)[0])
    wo = np.ascontiguousarray(np.asarray(inputs["w_out"], np.float32)[0])
    g1 = np.ascontiguousarray(np.asarray(inputs["norm_gain"], np.float32)[0].reshape(8, 128).T)
    gq = np.ascontiguousarray(np.asarray(inputs["q_a_norm"], np.float32)[0].reshape(2, 128).T)
    gk = np.ascontiguousarray(np.asarray(inputs["kv_a_norm"], np.float32)[0].reshape(128, 1))
    sinks = np.ascontiguousarray(np.asarray(inputs["sinks"], np.float32)[0].reshape(1, 8))
    rb = np.ascontiguousarray(np.asarray(inputs["rel_bias"], np.float32).reshape(1, 256))
    fg = np.ascontiguousarray(np.asarray(inputs["final_norm"], np.float32).reshape(1, D_MODEL))
    consts = np.zeros((128, 8), np.float32)
    pidx = np.arange(128)
    consts[:, 0] = (10000.0 ** (-((pidx % 16).astype(np.float64)) / 16.0)).astype(np.float32)
    consts[:, 1] = np.where((pidx % 32) < 16, -1.0, 1.0)
    in_maps = []
    for c in range(2 * B):
        b, p = c // 2, c % 2
        xb = x[b]
        xT = np.ascontiguousarray(xb.T)
        xTo = np.zeros((D_MODEL, NSL * 640), np.float32)
        xo = np.zeros((NSL * 512, D_MODEL), np.float32)
        po = np.zeros((1, NSL * 512), np.int32)
        valid = np.ones((128, NSL), np.float32)
        for j in range(NSL):
            ch = chunk_of(j, p)
            lo = ch * 512
            xo[j * 512:(j + 1) * 512] = xb[lo:lo + 512]
            po[0, j * 512:(j + 1) * 512] = pos[b, lo:lo + 512]
            xTo[:, j * 640 + 128:(j + 1) * 640] = xT[:, lo:lo + 512]
            if lo >= 128:
                xTo[:, j * 640:j * 640 + 128] = xT[:, lo - 128:lo]
            else:
                valid[:, j] = 0.0
        c1 = chunk_of(1, p) * 512
        pb = np.ascontiguousarray(pos[b:b + 1, c1 - 128:c1 + 128])
        in_maps.append({
            "xT_full": xT, "xT_own": xTo, "x_own": xo,
            "pos_full": np.ascontiguousarray(pos[b:b + 1]), "pos_own": po, "pos_bias": pb,
            "valid": valid, "masks": make_masks(p), "consts": consts,
            "g1": g1, "gq": gq, "gk": gk, "w1": w1, "w2": w2, "w3": w3,
            "wqb": np.ascontiguousarray(wqb), "wqbs": np.ascontiguousarray(wqbs), "wkvb": wkvb, "wo": wo,
            "sinks": sinks, "rb": rb, "fg": fg,
        })
    return in_maps


def assemble(results, B, S):
    NSL = S // 1024
    out = np.zeros((B, S, D_MODEL), np.float32)
    for c in range(2 * B):
        b, p = c // 2, c % 2
        oc = np.asarray(results[c]["out"])
        for j in range(NSL):
            ch = chunk_of(j, p)
            out[b, ch * 512:(ch + 1) * 512] = oc[j * 512:(j + 1) * 512]
    return out


def run(inputs, B, S):
    nc = build(S)
    in_maps = prep_inputs(inputs, B, S)
    res = run_bass_kernel_spmd(nc, in_maps, core_ids=list(range(2 * B)))
    return assemble(res.results, B, S)


def kernel(**inputs):
    return run(inputs, 4, 8192)
```

```python
import math
import types
from contextlib import ExitStack

import numpy as np
import concourse.bass as bass
import concourse.mybir as mybir
from concourse.bass_utils import run_bass_kernel_spmd

F32 = mybir.dt.float32
BF16 = mybir.dt.bfloat16
I32 = mybir.dt.int32
AF = mybir.ActivationFunctionType
ALU = mybir.AluOpType

D_MODEL = 1024
NKC = 8
IN_WIDTH = 2208
EPS = 1e-6
NEG = -30000.0
SBUF_BASE = 17408
SBUF_END = 229376
ENGS = ("pe", "act", "dve", "pool", "sp")
TWO_PI = 2.0 * math.pi
C1 = 6.28125
C2 = TWO_PI - C1
SKIP = set()
STOP_AFTER = None


class Buf:
    __slots__ = ("name", "w", "r", "sem", "semval")

    def __init__(self, name):
        self.name = name
        self.w = None
        self.r = []
        self.sem = None
        self.semval = 0


def _freeze(fn):
    if fn is None or fn.__closure__ is None:
        return fn
    cells = []
    for c in fn.__closure__:
        try:
            cells.append(types.CellType(c.cell_contents))
        except ValueError:
            cells.append(c)
    return types.FunctionType(fn.__code__, fn.__globals__, fn.__name__, fn.__defaults__, tuple(cells))


class Prog:
    def __init__(self):
        self.ops = {e: [] for e in ENGS}
        self.cnt = {e: 0 for e in ENGS}
        self.seen = {e: {} for e in ENGS}
        self.dma_bufs = []
        self.n_dma_sems = 0

    def _waits(self, eng, deps):
        need = {}
        for (k, v) in deps:
            if k == eng and eng == "pe":
                continue
            if v > need.get(k, 0):
                need[k] = v
        out = []
        seen = self.seen[eng]
        for k, v in need.items():
            if seen.get(k, 0) < v:
                seen[k] = v
                out.append((k, v))
        return out

    @staticmethod
    def _deps(reads, writes):
        deps = []
        for b in reads:
            if b.w is not None:
                deps.append(b.w)
        for b in writes:
            if b.w is not None:
                deps.append(b.w)
            deps.extend(b.r)
        return deps

    @staticmethod
    def _upd(tok, reads, writes):
        for b in reads:
            b.r.append(tok)
            if len(b.r) > 64:
                best = {}
                for (k, v) in b.r:
                    if v > best.get(k, 0):
                        best[k] = v
                b.r = list(best.items())
        for b in writes:
            b.w = tok
            b.r = []

    def op(self, eng, fn, reads=(), writes=()):
        waits = self._waits(eng, self._deps(reads, writes))
        self.cnt[eng] += 1
        tok = (eng, self.cnt[eng])
        self.ops[eng].append((_freeze(fn), waits, (eng, 1)))
        self._upd(tok, reads, writes)

    def dma(self, fn, reads=(), writes=(), q="sp"):
        waits = self._waits(q, self._deps(reads, writes))
        carrier = (list(writes) + list(reads))[0]
        if carrier.sem is None:
            carrier.sem = ("dma", self.n_dma_sems)
            self.n_dma_sems += 1
            self.dma_bufs.append(carrier)
        carrier.semval += 16
        tok = (carrier.sem, carrier.semval)
        self.ops[q].append((_freeze(fn), waits, (carrier.sem, 16)))
        self._upd(tok, reads, writes)

    def barrier(self):
        toks = [(e, self.cnt[e]) for e in ENGS if e != "sp" and self.cnt[e] > 0]
        toks += [(b.sem, b.semval) for b in self.dma_bufs if b.semval > 0]
        for e in ENGS:
            waits = self._waits(e, [t for t in toks if t[0] != e])
            if waits:
                self.ops[e].append((None, waits, None))

    def emit(self, nc, stack):
        sems = {}
        for e in ENGS:
            if e != "sp":
                sems[e] = stack.enter_context(nc.semaphore("s_" + e))
        for i in range(self.n_dma_sems):
            sems[("dma", i)] = stack.enter_context(nc.semaphore("s_dma%d" % i))
        block = stack.enter_context(nc.Block())

        def run(engname):
            def body(engine):
                for fn, waits, inc in self.ops[engname]:
                    for k, v in waits:
                        engine.wait_ge(sems[k], v)
                    if fn is not None:
                        fn(engine).then_inc(sems[inc[0]], inc[1])
            return body

        block.tensor(run("pe"))
        block.scalar(run("act"))
        block.vector(run("dve"))
        block.gpsimd(run("pool"))
        block.sync(run("sp"))


def t5_thresholds():
    n = np.arange(0, 512)
    nf = np.maximum(n, 1).astype(np.float32)
    large = 16 + (np.log(nf / np.float32(16)) / np.float32(math.log(128 / 16)) * np.float32(16)).astype(np.int32)
    large = np.minimum(large, 31)
    b = np.where(n < 16, n, large)
    return [int(np.min(n[b >= k])) for k in range(1, 32)]


def build(S):
    NCH = S // 512
    NSL = NCH // 2
    TO = NSL * 512
    NKB = S // 128
    NT1 = S // 512
    QWK = S // 4
    QWQ = TO // 4
    assert QWQ >= 512
    SCALE = float((64 + 32) ** -0.5)

    nc = bass.Bass("TRN2", target_bir_lowering=False)

    def din(name, shape, dt=F32):
        return nc.dram_tensor(name, list(shape), dt, kind="ExternalInput").ap()

    xT_full = din("xT_full", [D_MODEL, S])
    xT_own = din("xT_own", [D_MODEL, NSL * 640])
    x_own = din("x_own", [TO, D_MODEL])
    pos_full = din("pos_full", [1, S], I32)
    pos_own = din("pos_own", [1, TO], I32)
    pos_bias = din("pos_bias", [1, 256], I32)
    valid_in = din("valid", [128, NSL])
    masks_in = din("masks", [16, 128, 512])
    consts_in = din("consts", [128, 8])
    g1_in = din("g1", [128, 8])
    gq_in = din("gq", [128, 2])
    gk_in = din("gk", [128, 1])
    w1_in = din("w1", [D_MODEL, 320])
    w2_in = din("w2", [D_MODEL, 256])
    w3_in = din("w3", [D_MODEL, 1792])
    wqb_in = din("wqb", [256, 768])
    wqbs_in = din("wqbs", [256, 768])
    wkvb_in = din("wkvb", [128, 1024])
    wo_in = din("wo", [D_MODEL, D_MODEL])
    sinks_in = din("sinks", [1, 8])
    rb_in = din("rb", [1, 256])
    fg_in = din("fg", [1, D_MODEL])
    out_d = nc.dram_tensor("out", [TO, D_MODEL], F32, kind="ExternalOutput").ap()

    P = Prog()
    uid = [0]

    def SB(name, shape, dt, off):
        assert off % 32 == 0, (name, off)
        nb = int(np.prod(shape[1:])) * (2 if dt == BF16 else 4)
        assert SBUF_BASE <= off and off + nb <= SBUF_END, (name, off, nb)
        uid[0] += 1
        return nc.alloc_sbuf_tensor_at("%s_%d" % (name, uid[0]), list(shape), dt, offset=off)

    class Arena:
        def __init__(self, start, end):
            self.p = start
            self.end = end

        def get(self, name, shape, dt):
            nb = int(np.prod(shape[1:])) * (2 if dt == BF16 else 4)
            nb = (nb + 31) // 32 * 32
            t = SB(name, shape, dt, self.p)
            self.p += nb
            assert self.p <= self.end, (name, self.p, self.end)
            return t

    o = SBUF_BASE
    CONST0 = o; o += 20 * 1024
    BIAS0 = o; o += 8 * 1024
    W0 = o; o += 81 * 1024
    KVN0 = o; o += S * 2
    KT0 = o; o += S * 2
    QT0 = o; o += 8 * TO * 2
    QTEND = o
    assert o <= SBUF_END, o
    W1END = KVN0

    with ExitStack() as st:
        PSX = [st.enter_context(nc.psum_tensor("psx%d" % i, [128, 1024], F32)) for i in range(4)]

        def bank(i):
            return PSX[i // 2][:, (i % 2) * 512:(i % 2) * 512 + 512]

        def bank_t(i):
            return PSX[i // 2], (i % 2) * 512

        PB = [Buf("psb%d" % i) for i in range(8)]

        ca = Arena(CONST0, BIAS0)
        CONSTS = ca.get("consts", [128, 8], F32)
        G1 = ca.get("g1", [128, 8], F32)
        GQ = ca.get("gq", [128, 2], F32)
        GK = ca.get("gk", [128, 1], F32)
        EPSC = ca.get("eps", [128, 1], F32)
        PIH = ca.get("pih", [128, 1], F32)
        ESK = ca.get("esk", [128, 8], F32)
        VALID = ca.get("valid", [128, NSL], F32)
        IDENT = ca.get("ident", [128, 128], BF16)
        ONES = ca.get("ones", [128, 128], BF16)
        TQK = ca.get("tqk", [128, 128], F32)
        WKVB = ca.get("wkvb", [128, 1024], BF16)
        WQB = ca.get("wqb", [128, 2, 768], BF16)
        WQBS = ca.get("wqbs", [128, 2, 768], BF16)
        RB = ca.get("rb", [128, 32, 8], F32)
        DIFF = ca.get("diff", [128, 31, 8], F32)
        FG = ca.get("fg", [128, D_MODEL], F32)
        bC = Buf("consts")

        YT = SB("yt", [128, 4, TO], BF16, W0)
        bYT = [Buf("yt%d" % j) for j in range(NSL)]
        BIAS = [SB("bias%d" % t, [128, 8, 128], F32, BIAS0 + t * 4096) for t in range(2)]
        bBIAS = Buf("bias")
        KVN = SB("kvn", [128, S], BF16, KVN0)
        bKVN = Buf("kvn")
        KT = SB("kt", [128, S], BF16, KT0)
        bKTt = [Buf("ktn%d" % t) for t in range(NT1)]
        bKTr = Buf("ktr")
        QT = SB("qt", [128, 8, TO], BF16, QT0)
        bQT = [Buf("qt%d" % h) for h in range(8)]

        wa = Arena(W0, W1END)
        XT0 = wa.get("xt", [128, 8, 512], F32)
        HTs = [wa.get("ht%d" % i, [128, 8, 512], BF16) for i in range(2)]; bHTs = [Buf("ht0"), Buf("ht1")]
        bXTs = [Buf("xt0"), Buf("xt1")]
        RS = wa.get("rs", [128, 512], F32); bRS = Buf("rs")
        LNT = wa.get("lnt", [128, 512], F32); bLNT = Buf("lnt")
        SQ2 = wa.get("sq2", [128, 2, 512], BF16); bSQ2 = Buf("sq2")
        RS2 = wa.get("rs2", [128, 512], F32); bRS2 = Buf("rs2")
        LNT2 = wa.get("lnt2", [128, 512], F32); bLNT2 = Buf("lnt2")
        T1 = wa.get("t1", [128, 512], F32); bT1 = Buf("t1")
        T2 = wa.get("t2", [128, 512], F32); bT2 = Buf("t2")
        QN = wa.get("qn", [128, 2, 512], BF16); bQN = Buf("qn")
        W2 = wa.get("w2", [128, 8, 256], BF16)
        QTAB = [wa.get("qtab%d" % i, [128, QWQ], F32) for i in range(2)]
        stg_off = wa.p
        XT1 = wa.get("xt1", [128, 8, 512], F32)
        XTs = [XT0, XT1]
        STG = [SB("stg%d" % i, [128, 1792], F32, stg_off + i * 7168) for i in range(2)]
        bSTG = [Buf("stg0"), Buf("stg1")]
        bQTAB = Buf("qtab")
        qa = Arena(QT0, SBUF_END)
        W1 = qa.get("w1", [128, 8, 320], BF16)
        KTAB = [qa.get("ktab%d" % i, [128, QWK], F32) for i in range(2)]
        bKTAB = Buf("ktab")
        POSI = qa.get("posi", [128, QWK], I32)
        POSF = qa.get("posf", [128, QWK], F32)
        ANG = qa.get("ang", [128, QWK], F32)
        KI = qa.get("ki", [128, QWK], I32)
        KF = qa.get("kf", [128, QWK], F32)
        bPOSI, bPOSF, bANG, bKI, bKF = Buf("posi"), Buf("posf"), Buf("ang"), Buf("ki"), Buf("kf")
        TI = qa.get("ti", [128, 128], I32)
        bBW = Buf("biaswork")

        P.dma(lambda e: e.dma_start(out=CONSTS[:], in_=consts_in), writes=[bC])
        P.dma(lambda e: e.dma_start(out=G1[:], in_=g1_in), writes=[bC])
        P.dma(lambda e: e.dma_start(out=GQ[:], in_=gq_in), writes=[bC])
        P.dma(lambda e: e.dma_start(out=GK[:], in_=gk_in), writes=[bC])
        P.dma(lambda e: e.dma_start(out=VALID[:], in_=valid_in), writes=[bC])
        P.dma(lambda e: e.dma_start(out=ESK[:], in_=sinks_in.partition_broadcast(128)), writes=[bC])
        P.dma(lambda e: e.dma_start(out=RB[:], in_=rb_in.partition_broadcast(128)), writes=[bC])
        P.dma(lambda e: e.dma_start(out=FG[:], in_=fg_in.partition_broadcast(128)), writes=[bC])
        P.op("pool", lambda e: e.memset(EPSC[:], EPS), writes=[bC])
        P.op("pool", lambda e: e.memset(PIH[:], math.pi / 2), writes=[bC])
        P.op("pool", lambda e: e.memset(ONES[:], 1.0), writes=[bC])
        P.op("pool", lambda e: e.iota(TI[:], pattern=[[1, 128]], base=0, channel_multiplier=-1), writes=[bBW])
        P.op("pool", lambda e: e.tensor_copy(out=TQK[:], in_=TI[:]), reads=[bBW], writes=[bC])
        P.op("pool", lambda e: e.tensor_scalar(out=IDENT[:], in0=TQK[:], scalar1=0.0, scalar2=None, op0=ALU.is_equal),
             reads=[bC], writes=[bC])
        P.op("act", lambda e: e.activation(out=ESK[:], in_=ESK[:], func=AF.Exp), reads=[bC], writes=[bC])

        stg_i = [0]

        def load_w(src_ap, ncols, dst_ap, gain_col, eng="dve"):
            i = stg_i[0] % 2
            stg_i[0] += 1
            sv = STG[i][:, 0:ncols]
            P.dma(lambda e: e.dma_start(out=sv, in_=src_ap), writes=[bSTG[i]])
            P.op(eng, lambda e: e.tensor_scalar(out=dst_ap, in0=sv, scalar1=gain_col, scalar2=None, op0=ALU.mult),
                 reads=[bSTG[i], bC], writes=[bC])

        for kc in range(NKC):
            load_w(w1_in[kc * 128:(kc + 1) * 128, :], 320, W1[:, kc, :], G1[:, kc:kc + 1])
        load_w(wkvb_in, 1024, WKVB[:], GK[:, 0:1])

        def rope_tables(pos_ap, QW, TAB, bTAB):
            for q in range(4):
                P.dma(lambda e, q=q: e.dma_start(out=POSI[32 * q:32 * q + 32, 0:QW],
                                                 in_=pos_ap[:, q * QW:(q + 1) * QW].partition_broadcast(32)),
                      writes=[bPOSI])
            P.op("dve", lambda e: e.tensor_copy(out=POSF[:, 0:QW], in_=POSI[:, 0:QW]), reads=[bPOSI], writes=[bPOSF])
            A_, KI_, KF_ = ANG[:, 0:QW], KI[:, 0:QW], KF[:, 0:QW]
            for ti in range(2):
                if ti == 0:
                    P.op("dve", lambda e: e.tensor_scalar(out=A_, in0=POSF[:, 0:QW], scalar1=CONSTS[:, 0:1], scalar2=PIH[:, 0:1],
                                                          op0=ALU.mult, op1=ALU.add), reads=[bPOSF, bC], writes=[bANG])
                else:
                    P.op("dve", lambda e: e.tensor_scalar(out=A_, in0=POSF[:, 0:QW], scalar1=CONSTS[:, 0:1], scalar2=None,
                                                          op0=ALU.mult), reads=[bPOSF, bC], writes=[bANG])
                P.op("dve", lambda e: e.tensor_scalar(out=KI_, in0=A_, scalar1=1.0 / TWO_PI, scalar2=None, op0=ALU.mult),
                     reads=[bANG], writes=[bKI])
                P.op("dve", lambda e: e.tensor_copy(out=KF_, in_=KI_), reads=[bKI], writes=[bKF])
                P.op("dve", lambda e: e.scalar_tensor_tensor(out=A_, in0=KF_, scalar=-C1, in1=A_, op0=ALU.mult, op1=ALU.add),
                     reads=[bKF, bANG], writes=[bANG])
                P.op("dve", lambda e: e.scalar_tensor_tensor(out=A_, in0=KF_, scalar=-C2, in1=A_, op0=ALU.mult, op1=ALU.add),
                     reads=[bKF, bANG], writes=[bANG])
                P.op("dve", lambda e: e.tensor_scalar(out=KF_, in0=A_, scalar1=math.pi, scalar2=-TWO_PI, op0=ALU.is_gt, op1=ALU.mult),
                     reads=[bANG], writes=[bKF])
                P.op("dve", lambda e: e.tensor_tensor(out=A_, in0=A_, in1=KF_, op=ALU.add), reads=[bANG, bKF], writes=[bANG])
                P.op("dve", lambda e: e.tensor_scalar(out=KF_, in0=A_, scalar1=-math.pi, scalar2=TWO_PI, op0=ALU.is_lt, op1=ALU.mult),
                     reads=[bANG], writes=[bKF])
                P.op("dve", lambda e: e.tensor_tensor(out=A_, in0=A_, in1=KF_, op=ALU.add), reads=[bANG, bKF], writes=[bANG])
                P.op("dve", lambda e: e.tensor_scalar(out=A_, in0=A_, scalar1=3.14159, scalar2=-3.14159, op0=ALU.min, op1=ALU.max),
                     reads=[bANG], writes=[bANG])
                if ti == 0:
                    P.op("act", lambda e: e.activation(out=TAB[0][:, 0:QW], in_=A_, func=AF.Sin), reads=[bANG], writes=[bTAB])
                else:
                    P.op("act", lambda e: e.activation(out=TAB[1][:, 0:QW], in_=A_, func=AF.Sin, scale=CONSTS[:, 1:2]),
                         reads=[bANG, bC], writes=[bTAB])

        for kc in range(NKC):
            load_w(w2_in[kc * 128:(kc + 1) * 128, :], 256, W2[:, kc, :], G1[:, kc:kc + 1])
        for a in range(2):
            load_w(wqb_in[a * 128:(a + 1) * 128, :], 768, WQB[:, a, :], GQ[:, a:a + 1])
            load_w(wqbs_in[a * 128:(a + 1) * 128, :], 768, WQBS[:, a, :], GQ[:, a:a + 1])
        rope_tables(pos_full, QWK, KTAB, bKTAB)
        P.barrier()
        if STOP_AFTER == 'p0':
            P.emit(nc, st)
            return nc
        rope_tables(pos_own, QWQ, QTAB, bQTAB)

        def norm_tile(src_ap, XTt, bXTt, HTt, bHTt, RSt, LNTt, ntok, ss_psx, bss):
            P.dma(lambda e: e.dma_start(out=XTt[:, :, 0:ntok], in_=src_ap.rearrange("(kc p) t -> p kc t", p=128)), writes=[bXTt])
            P.op("act", lambda e: e.activation(out=HTt[:, :, 0:ntok], in_=XTt[:, :, 0:ntok], func=AF.Square), reads=[bXTt], writes=[bHTt])
            for (a, b_) in ((0, min(ntok, 512)), (512, ntok)):
                if b_ <= a:
                    continue
                for kc in range(NKC):
                    P.op("pe", lambda e, kc=kc, a=a, b_=b_: e.matmul(ss_psx[:, a:b_], lhsT=ONES[:], rhs=HTt[:, kc, a:b_],
                                                                   start=(kc == 0), stop=(kc == NKC - 1)),
                         reads=[bHTt, bC], writes=bss)
            P.op("act", lambda e: e.activation(out=LNTt[:, 0:ntok], in_=ss_psx[:, 0:ntok], func=AF.Ln, bias=EPSC[:, 0:1], scale=1.0 / D_MODEL),
                 reads=bss + [bC], writes=[bLNT])
            P.op("act", lambda e: e.activation(out=RSt[:, 0:ntok], in_=LNTt[:, 0:ntok], func=AF.Exp, scale=-0.5), reads=[bLNT], writes=[bRS])
            P.op("dve", lambda e: e.tensor_tensor(out=HTt[:, :, 0:ntok], in0=XTt[:, :, 0:ntok],
                                                  in1=RSt[:, 0:ntok].unsqueeze(1).to_broadcast([128, 8, ntok]), op=ALU.mult),
                 reads=[bXTt, bRS], writes=[bHTt])

        def p1_norm(t):
            norm_tile(xT_full[:, t * 512:t * 512 + 512], XTs[t % 2], bXTs[t % 2], HTs[t % 2], bHTs[t % 2], RS, LNT, 512, PSX[0], [PB[0]])

        def p1_main(t):
            c0 = t * 512
            HT = HTs[t % 2]
            bHT = bHTs[t % 2]
            for (bk, lo, hi, M) in ((2, 0, 128, 128), (3, 128, 224, 96), (4, 224, 320, 96)):
                for kc in range(NKC):
                    P.op("pe", lambda e, kc=kc, bk=bk, lo=lo, hi=hi, M=M: e.matmul(bank(bk)[0:M, :], lhsT=W1[:, kc, lo:hi], rhs=HT[:, kc, :],
                                                                                start=(kc == 0), stop=(kc == NKC - 1)),
                         reads=[bHT, bC], writes=[PB[bk]])
            P.op("act", lambda e: e.activation(out=SQ2[:, 0, :], in_=bank(2), func=AF.Square), reads=[PB[2]], writes=[bSQ2])
            P.op("pe", lambda e: e.matmul(bank(5), lhsT=ONES[:], rhs=SQ2[:, 0, :], start=True, stop=True), reads=[bSQ2, bC], writes=[PB[5]])
            P.op("act", lambda e: e.activation(out=LNT2[:], in_=bank(5), func=AF.Ln, bias=EPSC[:, 0:1], scale=1.0 / 128), reads=[PB[5], bC], writes=[bLNT2])
            P.op("act", lambda e: e.activation(out=RS2[:], in_=LNT2[:], func=AF.Exp, scale=-0.5), reads=[bLNT2], writes=[bRS2])
            P.op("dve", lambda e, c0=c0: e.tensor_tensor(out=KVN[:, c0:c0 + 512], in0=bank(2), in1=RS2[:], op=ALU.mult),
                 reads=[PB[2], bRS2], writes=[bKVN])
            q = c0 // QWK
            off = c0 % QWK
            P.op("dve", lambda e, q=q, off=off: e.tensor_tensor(out=T1[64:96, :], in0=bank(3)[64:96, :], in1=KTAB[0][32 * q:32 * q + 32, off:off + 512], op=ALU.mult),
                 reads=[PB[3], bKTAB], writes=[bT1])
            P.op("dve", lambda e, q=q, off=off: e.tensor_tensor(out=T2[64:96, :], in0=bank(4)[64:96, :], in1=KTAB[1][32 * q:32 * q + 32, off:off + 512], op=ALU.mult),
                 reads=[PB[4], bKTAB], writes=[bT2])
            P.op("dve", lambda e, c0=c0: e.tensor_tensor(out=KT[64:96, c0:c0 + 512], in0=T1[64:96, :], in1=T2[64:96, :], op=ALU.add),
                 reads=[bT1, bT2], writes=[bKTr])

        p1_norm(0)
        for t in range(NT1):
            if t + 1 < NT1:
                p1_norm(t + 1)
            p1_main(t)

        P.barrier()
        if STOP_AFTER == 'p1':
            P.emit(nc, st)
            return nc

        def p2_norm(t):
            norm_tile(xT_own[:, t * 640 + 128:t * 640 + 640], XTs[t % 2], bXTs[t % 2], HTs[t % 2], bHTs[t % 2], RS, LNT, 512, PSX[0], [PB[0]])

        def p2_main(t):
            c0 = t * 512
            HT = HTs[t % 2]
            bHT = bHTs[t % 2]
            for m in range(2):
                for kc in range(NKC):
                    P.op("pe", lambda e, kc=kc, m=m: e.matmul(bank(2 + m), lhsT=W2[:, kc, m * 128:(m + 1) * 128], rhs=HT[:, kc, :],
                                                              start=(kc == 0), stop=(kc == NKC - 1)),
                         reads=[bHT, bC], writes=[PB[2 + m]])
            for m in range(2):
                P.op("act", lambda e, m=m: e.activation(out=SQ2[:, m, :], in_=bank(2 + m), func=AF.Square), reads=[PB[2 + m]], writes=[bSQ2])
            for m in range(2):
                P.op("pe", lambda e, m=m: e.matmul(bank(5), lhsT=ONES[:], rhs=SQ2[:, m, :], start=(m == 0), stop=(m == 1)), reads=[bSQ2, bC], writes=[PB[5]])
            P.op("act", lambda e: e.activation(out=LNT2[:], in_=bank(5), func=AF.Ln, bias=EPSC[:, 0:1], scale=1.0 / 256), reads=[PB[5], bC], writes=[bLNT2])
            P.op("act", lambda e: e.activation(out=RS2[:], in_=LNT2[:], func=AF.Exp, scale=-0.5), reads=[bLNT2], writes=[bRS2])
            for m in range(2):
                P.op("dve", lambda e, m=m: e.tensor_tensor(out=QN[:, m, :], in0=bank(2 + m), in1=RS2[:], op=ALU.mult),
                     reads=[PB[2 + m], bRS2], writes=[bQN])
            q = c0 // QWQ
            off = c0 % QWQ
            for h in range(8):
                bq = (6, 7, 2)[h % 3]
                bs = (4, 1, 3)[h % 3]
                for m in range(2):
                    P.op("pe", lambda e, m=m, h=h, bq=bq: e.matmul(bank(bq)[0:96, :], lhsT=WQB[:, m, h * 96:(h + 1) * 96], rhs=QN[:, m, :],
                                                                   start=(m == 0), stop=(m == 1)), reads=[bQN, bC], writes=[PB[bq]])
                for m in range(2):
                    P.op("pe", lambda e, m=m, h=h, bs=bs: e.matmul(bank(bs)[0:96, :], lhsT=WQBS[:, m, h * 96:(h + 1) * 96], rhs=QN[:, m, :],
                                                                   start=(m == 0), stop=(m == 1)), reads=[bQN, bC], writes=[PB[bs]])
                P.op("act", lambda e, h=h, bq=bq, c0=c0: e.activation(out=QT[0:64, h, c0:c0 + 512], in_=bank(bq)[0:64, :], func=AF.Copy, scale=SCALE),
                     reads=[PB[bq]], writes=[bQT[h]])
                P.op("dve", lambda e, bq=bq, q=q, off=off: e.scalar_tensor_tensor(out=T1[64:96, :], in0=bank(bq)[64:96, :], scalar=SCALE,
                                                                                in1=QTAB[0][32 * q:32 * q + 32, off:off + 512], op0=ALU.mult, op1=ALU.mult),
                     reads=[PB[bq], bQTAB], writes=[bT1])
                P.op("dve", lambda e, bs=bs, q=q, off=off: e.scalar_tensor_tensor(out=T2[64:96, :], in0=bank(bs)[64:96, :], scalar=SCALE,
                                                                                in1=QTAB[1][32 * q:32 * q + 32, off:off + 512], op0=ALU.mult, op1=ALU.mult),
                     reads=[PB[bs], bQTAB], writes=[bT2])
                P.op("dve", lambda e, h=h, c0=c0: e.tensor_tensor(out=QT[64:96, h, c0:c0 + 512], in0=T1[64:96, :], in1=T2[64:96, :], op=ALU.add),
                     reads=[bT1, bT2], writes=[bQT[h]])

        p2_norm(0)
        for t in range(NSL):
            if t + 1 < NSL:
                p2_norm(t + 1)
            p2_main(t)

        P.barrier()
        if STOP_AFTER == 'p2':
            P.emit(nc, st)
            return nc

        pa = Arena(W0 + 4 * TO * 2, W1END)
        VP = pa.get("vp", [128, NKB, 128], BF16); bVPg = [Buf("vp%d" % g) for g in range(NKB // 8)]
        MASKT = pa.get("maskt", [128, 16, 512], BF16); bMASK = Buf("mask")
        PT = [pa.get("pt%d" % i, [128, 512], BF16) for i in range(4)]
        bPT = [Buf("pt%d" % i) for i in range(4)]
        RC = pa.get("rc", [128, 512], F32); bRC = Buf("rc")
        tmpb_off = pa.p
        TMPB = pa.get("tmpb", [128, 8, 128], F32)
        MSTG = [SB("mstg%d" % i, [128, 512], F32, tmpb_off + i * 2048) for i in range(2)]
        bMSTG = [Buf("mstg0"), Buf("mstg1")]
        for i in range(16):
            k = i % 2
            P.dma(lambda e, i=i, k=k: e.dma_start(out=MSTG[k][:], in_=masks_in[i]), writes=[bMSTG[k]])
            P.op("dve", lambda e, i=i, k=k: e.tensor_copy(out=MASKT[:, i, :], in_=MSTG[k][:]), reads=[bMSTG[k]], writes=[bMASK])

        PB256 = pa.get("pb256", [128, 256], I32)
        PQF = pa.get("pqf", [128, 128], F32)
        KCOL = pa.get("kcol", [128, 2], I32)
        KCOLF = pa.get("kcolf", [128, 2], F32)
        DT_ = [pa.get("dt%d" % t, [128, 128], F32) for t in range(2)]
        IND = pa.get("ind", [128, 128], F32)
        MSK = pa.get("msk", [128, 128], F32)
        thr = t5_thresholds()
        P.dma(lambda e: e.dma_start(out=PB256[:], in_=pos_bias.partition_broadcast(128)), writes=[bBW])
        P.dma(lambda e: e.dma_start(out=KCOL[:, 0:1], in_=pos_bias[0:1, 0:128].rearrange("o (p f) -> p (o f)", f=1)), writes=[bBW])
        P.dma(lambda e: e.dma_start(out=KCOL[:, 1:2], in_=pos_bias[0:1, 128:256].rearrange("o (p f) -> p (o f)", f=1)), writes=[bBW])
        P.op("pool", lambda e: e.tensor_copy(out=PQF[:], in_=PB256[:, 128:256]), reads=[bBW], writes=[bBW])
        P.op("pool", lambda e: e.tensor_copy(out=KCOLF[:], in_=KCOL[:]), reads=[bBW], writes=[bBW])
        P.op("pool", lambda e: e.tensor_tensor(out=DIFF[:], in0=RB[:, 1:32, :], in1=RB[:, 0:31, :], op=ALU.subtract), reads=[bC, bBW], writes=[bBW])
        for t in range(2):
            P.op("pool", lambda e, t=t: e.tensor_scalar(out=DT_[t][:], in0=PQF[:], scalar1=KCOLF[:, t:t + 1], scalar2=None, op0=ALU.subtract),
                 reads=[bBW], writes=[bBW])
            P.op("pool", lambda e, t=t: e.tensor_copy(out=BIAS[t][:], in_=RB[:, 0, :].unsqueeze(2).to_broadcast([128, 8, 128])),
                 reads=[bC], writes=[bBIAS])
            for b in range(31):
                P.op("pool", lambda e, t=t, b=b: e.tensor_scalar(out=IND[:], in0=DT_[t][:], scalar1=float(thr[b]), scalar2=None, op0=ALU.is_ge),
                     reads=[bBW], writes=[bBW])
                P.op("pool", lambda e, b=b: e.tensor_tensor(out=TMPB[:], in0=IND[:].unsqueeze(1).to_broadcast([128, 8, 128]),
                                                            in1=DIFF[:, b, :].unsqueeze(2).to_broadcast([128, 8, 128]), op=ALU.mult),
                     reads=[bBW], writes=[bBW, bMSTG[0], bMSTG[1]])
                P.op("pool", lambda e, t=t: e.tensor_tensor(out=BIAS[t][:], in0=BIAS[t][:], in1=TMPB[:], op=ALU.add),
                     reads=[bBW, bBIAS], writes=[bBIAS])
            if t == 0:
                P.op("pool", lambda e: e.tensor_scalar(out=MSK[:], in0=TQK[:], scalar1=0.0, scalar2=NEG, op0=ALU.is_ge, op1=ALU.mult),
                     reads=[bC], writes=[bBW])
            else:
                P.op("pool", lambda e: e.tensor_scalar(out=MSK[:], in0=TQK[:], scalar1=0.0, scalar2=NEG, op0=ALU.is_lt, op1=ALU.mult),
                     reads=[bC], writes=[bBW])
            P.op("pool", lambda e, t=t: e.tensor_tensor(out=BIAS[t][:], in0=BIAS[t][:], in1=MSK[:].unsqueeze(1).to_broadcast([128, 8, 128]), op=ALU.add),
                 reads=[bBW, bBIAS], writes=[bBIAS])

        steps = [(h, j, kb) for h in range(8) for j in range(NSL) for kb in range(8 * j + 8)]
        LA = 3
        NR = 4

        def build_kv(h, jn):
            vo = 64 * (h % 2)
            so = 64 - vo
            for t in (2 * jn, 2 * jn + 1):
                bk = 6 + (t % 2)
                P.op("pe", lambda e, t=t, bk=bk: e.matmul(bank(bk)[0:64, :], lhsT=WKVB[:, h * 128:h * 128 + 64], rhs=KVN[:, t * 512:(t + 1) * 512],
                                                          start=True, stop=True), reads=[bKVN, bC], writes=[PB[bk]])
                P.op("dve", lambda e, t=t, bk=bk: e.tensor_copy(out=KT[0:64, t * 512:(t + 1) * 512], in_=bank(bk)[0:64, :]), reads=[PB[bk]], writes=[bKTt[t]])
            g8 = jn
            P.op("dve", lambda e: e.memset(VP[:, g8 * 8:(g8 + 1) * 8, so:so + 64], 1.0), writes=[bVPg[g8]])
            for bb in range(8):
                blk = g8 * 8 + bb
                P.op("pe", lambda e, blk=blk, bb=bb: e.matmul(bank(7)[:, bb * 64:(bb + 1) * 64], lhsT=KVN[:, blk * 128:(blk + 1) * 128],
                                                              rhs=WKVB[:, h * 128 + 64:h * 128 + 128], start=True, stop=True),
                     reads=[bKVN, bC], writes=[PB[7]])
            P.op("dve", lambda e: e.tensor_copy(out=VP[:, g8 * 8:(g8 + 1) * 8, vo:vo + 64], in_=bank(7).rearrange("p (a b) -> p a b", a=8)),
                 reads=[PB[7]], writes=[bVPg[g8]])

        def emit_qk(idx):
            h, j, kb = steps[idx]
            sb = idx % NR
            P.op("pe", lambda e: e.matmul(bank(sb), lhsT=KT[0:96, kb * 128:(kb + 1) * 128],
                                          rhs=QT[0:96, h, j * 512:(j + 1) * 512], start=True, stop=True),
                 reads=[bKTt[kb // 4], bKTr, bQT[h]], writes=[PB[sb]])

        build_kv(0, 0)
        for idx in range(min(LA, len(steps))):
            emit_qk(idx)
        XH = 16 if NSL >= 2 else 4
        for idx, (h, j, kb) in enumerate(steps):
            vo = 64 * (h % 2)
            so = 64 - vo
            if j + 1 < NSL and kb == max(0, 8 * j + 8 - 12):
                build_kv(h, j + 1)
            if j == NSL - 1 and kb == XH and h + 1 < 8:
                build_kv(h + 1, 0)
            if idx + LA < len(steps):
                emit_qk(idx + LA)
            sb = idx % NR
            ob = 4 + ((h * NSL + j) % 2)
            last = 8 * j + 7
            P.op("act", lambda e: e.activation(out=PT[sb][:], in_=bank(sb), func=AF.Exp), reads=[PB[sb]], writes=[bPT[sb]])
            if kb >= 8 * j:
                mi = (j % 2) * 8 + (kb - 8 * j)
                P.op("dve", lambda e: e.tensor_tensor(out=PT[sb][:], in0=PT[sb][:], in1=MASKT[:, mi, :], op=ALU.mult),
                     reads=[bPT[sb], bMASK], writes=[bPT[sb]])
            P.op("pe", lambda e: e.matmul(bank(ob), lhsT=VP[:, kb, :], rhs=PT[sb][:], start=(kb == 0), stop=(kb == last)),
                 reads=[bVPg[kb // 8], bPT[sb]], writes=[PB[ob]])
            if kb == last:
                P.op("dve", lambda e: e.reciprocal(out=RC[so:so + 64, :], in_=bank(ob)[so:so + 64, :]), reads=[PB[ob]], writes=[bRC])
                P.op("dve", lambda e: e.tensor_tensor(out=YT[vo:vo + 64, h // 2, j * 512:(j + 1) * 512],
                                                      in0=bank(ob)[vo:vo + 64, :], in1=RC[so:so + 64, :], op=ALU.mult),
                     reads=[PB[ob], bRC], writes=[bYT[j]])

        P.barrier()
        if STOP_AFTER == 'p2b':
            P.emit(nc, st)
            return nc

        p3 = Arena(W0 + 4 * TO * 2, SBUF_END)
        W3 = p3.get("w3", [128, 8, 1792], BF16)
        WO = p3.get("wo", [128, 8, 1024], BF16)
        xt3_off = p3.p
        XT3 = p3.get("xt3", [128, 8, 640], F32)
        HT3s = [p3.get("ht3_%d" % i, [128, 8, 640], BF16) for i in range(2)]; bHT3s = [Buf("ht3_0"), Buf("ht3_1")]
        gt1_off = p3.p
        GT1 = [p3.get("gt1_%d" % i, [128, 512], F32) for i in range(2)]; bGT1 = [Buf("gt1_0"), Buf("gt1_1")]
        RS3 = p3.get("rs3", [128, 640], F32)
        LNT3 = RS3
        bXT3 = Buf("xt3")
        QS = p3.get("qs", [128, 4, 512], BF16); bQS = Buf("qs")
        KS = p3.get("ks", [128, 640], BF16); bKS = Buf("ks")
        VS = p3.get("vs", [128, 5, 2, 128], BF16); bVS = Buf("vs")
        PTS = [[p3.get("pts%d_%d" % (p_, i), [128, 512], BF16) for i in range(2)] for p_ in range(2)]
        bPTS = [[Buf("pts%d_%d" % (p_, i)) for i in range(2)] for p_ in range(2)]
        RCS = [p3.get("rcs%d" % p_, [128, 512], F32) for p_ in range(2)]; bRCS = [Buf("rcs0"), Buf("rcs1")]
        YS = p3.get("ys", [128, 4, 512], BF16); bYS = Buf("ys")
        SGs = [p3.get("sg%d" % i, [128, 8, 512], BF16) for i in range(2)]; bSGs = [Buf("sg0"), Buf("sg1")]
        XO = [p3.get("xo%d" % i, [128, 1024], F32) for i in range(2)]; bXO = [Buf("xo0"), Buf("xo1")]
        RR = p3.get("rr", [128, 1024], F32); bRR = Buf("rr")
        JNK = SB("jnk", [128, 1024], BF16, gt1_off); bJNK = bGT1[0]
        ONEC = p3.get("onec", [128, 1], F32)
        OB = [p3.get("ob%d" % i, [128, 1024], F32) for i in range(2)]; bOB = [Buf("ob0"), Buf("ob1")]
        SS4 = p3.get("ss4", [128, 4], F32); bSS4 = Buf("ss4")
        STG3 = [SB("stg3_%d" % i, [128, 1792], F32, xt3_off + i * 7168) for i in range(2)]; bSTG3 = [Buf("stg3_0"), Buf("stg3_1")]
        BHLT = SB("bhlt", [128, 8, 512], BF16, xt3_off + 14336)
        BDT = SB("bdt", [128, 512], F32, xt3_off + 14336 + 8192)
        BHL = SB("bhl", [128, 8, 512], BF16, BIAS0)
        bBHL = Buf("bhl")
        bW3 = Buf("w3")

        for kc in range(NKC):
            i = kc % 2
            P.dma(lambda e, kc=kc, i=i: e.dma_start(out=STG3[i][:], in_=w3_in[kc * 128:(kc + 1) * 128, :]), writes=[bSTG3[i]])
            if kc % 2 == 0:
                P.op("dve", lambda e, kc=kc, i=i: e.tensor_scalar(out=W3[:, kc, :], in0=STG3[i][:], scalar1=G1[:, kc:kc + 1], scalar2=None, op0=ALU.mult),
                     reads=[bSTG3[i], bC], writes=[bW3])
            else:
                P.op("act", lambda e, kc=kc, i=i: e.activation(out=W3[:, kc, :], in_=STG3[i][:], func=AF.Copy, scale=G1[:, kc:kc + 1]),
                     reads=[bSTG3[i], bC], writes=[bW3])
        for kc in range(NKC):
            i = kc % 2
            P.dma(lambda e, kc=kc, i=i: e.dma_start(out=STG3[i][:, 0:1024], in_=wo_in[kc * 128:(kc + 1) * 128, :]), writes=[bSTG3[i]])
            if kc % 2 == 0:
                P.op("dve", lambda e, kc=kc, i=i: e.tensor_copy(out=WO[:, kc, :], in_=STG3[i][:, 0:1024]), reads=[bSTG3[i]], writes=[bW3])
            else:
                P.op("act", lambda e, kc=kc, i=i: e.activation(out=WO[:, kc, :], in_=STG3[i][:, 0:1024], func=AF.Copy), reads=[bSTG3[i]], writes=[bW3])
        P.op("pool", lambda e: e.memset(VS[:, :, :, 64:128], 1.0), writes=[bVS])
        bONE = Buf("onec")
        P.op("pool", lambda e: e.memset(ONEC[:], 1.0), writes=[bONE])
        for t in range(2):
            for g in range(2):
                k = t * 2 + g
                src = BIAS[t][:, 4 * g:4 * g + 4, :].rearrange("p a b -> p (a b)")
                P.op("pool", lambda e, k=k, src=src: e.tensor_copy(out=BHLT[:, k, :], in_=src), reads=[bBIAS], writes=[bBHL])
                P.op("pool", lambda e, k=k, src=src: e.tensor_tensor(out=BDT[:], in0=src, in1=BHLT[:, k, :], op=ALU.subtract), reads=[bBIAS, bBHL], writes=[bBHL])
                P.op("pool", lambda e, k=k: e.tensor_copy(out=BHLT[:, 4 + k, :], in_=BDT[:]), reads=[bBHL], writes=[bBHL])
        P.barrier()
        P.op("pool", lambda e: e.tensor_copy(out=BHL[:], in_=BHLT[:]), reads=[bBHL], writes=[bBHL])
        P.barrier()
        if STOP_AFTER == 'p3w':
            P.emit(nc, st)
            return nc

        def proj(HT3, bHT3, col0, ncol, bk, ntok=512, tok0=128):
            for kc in range(NKC):
                P.op("pe", lambda e, kc=kc: e.matmul(bank(bk)[0:ncol, 0:ntok], lhsT=W3[:, kc, col0:col0 + ncol], rhs=HT3[:, kc, tok0:tok0 + ntok],
                                                     start=(kc == 0), stop=(kc == NKC - 1)), reads=[bHT3, bW3], writes=[PB[bk]])

        def p3_pre(j):
            norm_tile(xT_own[:, j * 640:(j + 1) * 640], XT3, bXT3, HT3s[j % 2], bHT3s[j % 2], RS3, LNT3, 640, PSX[0], [PB[0], PB[1]])

        def p3_proj(j):
            HT3 = HT3s[j % 2]
            bHT3 = bHT3s[j % 2]
            for u in range(4):
                bk = 2 + (u % 2)
                proj(HT3, bHT3, u * 128, 128, bk)
                P.op("act", lambda e, u=u, bk=bk: e.activation(out=QS[:, u, :], in_=bank(bk), func=AF.Copy, scale=0.125), reads=[PB[bk]], writes=[bQS])
            proj(HT3, bHT3, 512, 128, 4, ntok=512, tok0=128)
            P.op("dve", lambda e: e.tensor_copy(out=KS[:, 128:640], in_=bank(4)), reads=[PB[4]], writes=[bKS])
            proj(HT3, bHT3, 512, 128, 5, ntok=128, tok0=0)
            P.op("dve", lambda e: e.tensor_copy(out=KS[:, 0:128], in_=bank(5)[:, 0:128]), reads=[PB[5]], writes=[bKS])
            for w in range(5):
                for kc in range(NKC):
                    P.op("pe", lambda e, kc=kc, w=w: e.matmul(bank(6)[:, w * 128:(w + 1) * 128] if w < 4 else bank(7)[:, 0:128],
                                                              lhsT=HT3[:, kc, w * 128:(w + 1) * 128], rhs=W3[:, kc, 640:768],
                                                              start=(kc == 0), stop=(kc == NKC - 1)), reads=[bHT3, bW3], writes=[PB[6] if w < 4 else PB[7]])
            P.op("dve", lambda e: e.tensor_copy(out=VS[:, 0:4, :, 0:64], in_=bank(6).rearrange("p (w g d) -> p w g d", w=4, g=2)),
                 reads=[PB[6]], writes=[bVS])
            P.op("dve", lambda e: e.tensor_copy(out=VS[:, 4, :, 0:64], in_=bank(7)[:, 0:128].rearrange("p (g d) -> p g d", g=2)),
                 reads=[PB[7]], writes=[bVS])
            P.op("dve", lambda e: e.tensor_scalar(out=VS[:, 0, :, 64:128], in0=ONES[:].rearrange("p (g d) -> p g d", g=2),
                                                  scalar1=VALID[:, j:j + 1], scalar2=None, op0=ALU.mult), reads=[bC], writes=[bVS])

        def gate_tile(j, gt):
            HT3 = HT3s[j % 2]
            bHT3 = bHT3s[j % 2]
            SG = SGs[j % 2]
            bSG = bSGs[j % 2]
            bk = gt % 2
            gp = gt % 2
            proj(HT3, bHT3, 768 + gt * 128, 128, bk)
            P.op("act", lambda e: e.activation(out=GT1[gp][:], in_=bank(bk), func=AF.Exp, scale=-1.0), reads=[PB[bk]], writes=[bGT1[gp]])
            P.op("act", lambda e: e.activation(out=GT1[gp][:], in_=GT1[gp][:], func=AF.Ln, bias=ONEC[:, 0:1]), reads=[bGT1[gp], bONE], writes=[bGT1[gp]])
            P.op("act", lambda e: e.activation(out=GT1[gp][:], in_=GT1[gp][:], func=AF.Exp, scale=-1.0), reads=[bGT1[gp]], writes=[bGT1[gp]])
            P.op("dve", lambda e: e.tensor_tensor(out=SG[:, gt, :], in0=bank(bk), in1=GT1[gp][:], op=ALU.mult), reads=[PB[bk], bGT1[gp]], writes=[bSG])

        def swa_a(j, n_it):
            i, g = n_it // 2, n_it % 2
            pn = n_it % 2
            ob = 6 + pn
            for t in range(2):
                w = i + t
                sbk = 2 + 2 * pn + t
                k = t * 2 + g
                P.op("pe", lambda e, w=w, sbk=sbk: e.matmul(bank(sbk), lhsT=KS[64 * g:64 * g + 64, w * 128:(w + 1) * 128],
                                                          rhs=QS[64 * g:64 * g + 64, :, i * 128:(i + 1) * 128], start=True, stop=False),
                     reads=[bKS, bQS], writes=[PB[sbk]])
                P.op("pe", lambda e, sbk=sbk, k=k: e.matmul(bank(sbk), lhsT=IDENT[:], rhs=BHL[:, k, :], start=False, stop=False),
                     reads=[bBHL, bC], writes=[PB[sbk]])
                P.op("pe", lambda e, sbk=sbk, k=k: e.matmul(bank(sbk), lhsT=IDENT[:], rhs=BHL[:, 4 + k, :], start=False, stop=True),
                     reads=[bBHL, bC], writes=[PB[sbk]])
                P.op("act", lambda e, t=t, sbk=sbk: e.activation(out=PTS[pn][t][:], in_=bank(sbk), func=AF.Exp), reads=[PB[sbk]], writes=[bPTS[pn][t]])

        def swa_b(j, n_it):
            i, g = n_it // 2, n_it % 2
            pn = n_it % 2
            ob = 6 + pn
            for t in range(2):
                w = i + t
                P.op("pe", lambda e, t=t, w=w: e.matmul(bank(ob), lhsT=VS[:, w, g, :], rhs=PTS[pn][t][:], start=(t == 0), stop=(t == 1)),
                     reads=[bVS, bPTS[pn][t]], writes=[PB[ob]])
            P.op("dve", lambda e: e.tensor_tensor(out=RCS[pn][64:128, :].rearrange("p (a b) -> p a b", a=4),
                                                  in0=bank(ob)[64:128, :].rearrange("p (a b) -> p a b", a=4),
                                                  in1=ESK[64:128, 4 * g:4 * g + 4].unsqueeze(2).to_broadcast([64, 4, 128]), op=ALU.add),
                 reads=[PB[ob], bC], writes=[bRCS[pn]])
            P.op("act", lambda e: e.activation(out=RCS[pn][64:128, :], in_=RCS[pn][64:128, :], func=AF.Ln), reads=[bRCS[pn]], writes=[bRCS[pn]])
            P.op("act", lambda e: e.activation(out=RCS[pn][64:128, :], in_=RCS[pn][64:128, :], func=AF.Exp, scale=-1.0), reads=[bRCS[pn]], writes=[bRCS[pn]])
            for u in range(4):
                po = 64 * (u % 2)
                tl = 2 * g + u // 2
                P.op("dve", lambda e, u=u, po=po, tl=tl: e.tensor_tensor(out=YS[po:po + 64, tl, i * 128:(i + 1) * 128],
                                                                         in0=bank(ob)[0:64, u * 128:(u + 1) * 128],
                                                                         in1=RCS[pn][64:128, u * 128:(u + 1) * 128], op=ALU.mult),
                     reads=[PB[ob], bRCS[pn]], writes=[bYS])

        def yg(j):
            SG = SGs[j % 2]
            bSG = bSGs[j % 2]
            P.op("dve", lambda e: e.tensor_tensor(out=SG[:, 0:4, :], in0=YT[:, :, j * 512:(j + 1) * 512], in1=SG[:, 0:4, :], op=ALU.mult),
                 reads=[bYT[j], bSG], writes=[bSG])
            P.op("dve", lambda e: e.tensor_tensor(out=SG[:, 4:8, :], in0=YS[:], in1=SG[:, 4:8, :], op=ALU.mult),
                 reads=[bYS, bSG], writes=[bSG])

        def out_block(j, i):
            SG = SGs[j % 2]
            bSG = bSGs[j % 2]
            row0 = j * 512 + i * 128
            k = i % 2
            P.dma(lambda e: e.dma_start(out=XO[k][:], in_=x_own[row0:row0 + 128, :]), writes=[bXO[k]])
            for hf in range(2):
                bk = hf
                for kc in range(NKC):
                    P.op("pe", lambda e, kc=kc, hf=hf, bk=bk: e.matmul(bank(bk), lhsT=SG[:, kc, i * 128:(i + 1) * 128], rhs=WO[:, kc, hf * 512:(hf + 1) * 512],
                                                                       start=(kc == 0), stop=(kc == NKC - 1)), reads=[bSG, bW3], writes=[PB[bk]])
                P.op("dve", lambda e, hf=hf, bk=bk: e.tensor_tensor(out=RR[:, hf * 512:(hf + 1) * 512], in0=bank(bk), in1=XO[k][:, hf * 512:(hf + 1) * 512], op=ALU.add),
                     reads=[PB[bk], bXO[k]], writes=[bRR])
            P.op("act", lambda e: e.activation(out=JNK[:], in_=RR[:], func=AF.Square, accum_out=SS4[:, 0:1]), reads=[bRR], writes=[bJNK, bSS4])
            P.op("act", lambda e: e.activation(out=SS4[:, 1:2], in_=SS4[:, 0:1], func=AF.Ln, bias=EPSC[:, 0:1], scale=1.0 / D_MODEL), reads=[bSS4, bC], writes=[bSS4])
            P.op("act", lambda e: e.activation(out=SS4[:, 2:3], in_=SS4[:, 1:2], func=AF.Exp, scale=-0.5), reads=[bSS4], writes=[bSS4])
            P.op("dve", lambda e: e.scalar_tensor_tensor(out=OB[k][:], in0=RR[:], scalar=SS4[:, 2:3], in1=FG[:], op0=ALU.mult, op1=ALU.mult),
                 reads=[bRR, bSS4, bC], writes=[bOB[k]])
            P.dma(lambda e: e.dma_start(out=out_d[row0:row0 + 128, :], in_=OB[k][:]), reads=[bOB[k]])

        p3_pre(0)
        p3_proj(0)
        for j in range(NSL + 1):
            if j + 1 < NSL:
                p3_pre(j + 1)
            for n_it in range(8):
                if j >= 1 and n_it % 2 == 0:
                    out_block(j - 1, n_it // 2)
                if j < NSL:
                    gate_tile(j, n_it)
                    if n_it == 0:
                        swa_a(j, 0)
                    if n_it + 1 < 8:
                        swa_a(j, n_it + 1)
                    swa_b(j, n_it)
            if j < NSL:
                yg(j)
            if j + 1 < NSL:
                p3_proj(j + 1)

        P.barrier()
        P.emit(nc, st)
    return nc


def chunk_of(j, p):
    return 2 * j + ((j + p) % 2)


def make_masks(p):
    m = np.zeros((2, 8, 128, 512), np.float32)
    k = np.arange(128)[:, None]
    q = np.arange(512)[None, :]
    for s in range(2):
        delta = (s + p) % 2
        for r in range(8):
            vis = (r * 128 + k) <= (delta * 512 + q)
            m[s, r] = np.where(vis, 1.0, 0.0)
    return m.reshape(16, 128, 512)


def prep_inputs(inputs, B, S):
    NSL = S // 1024
    x = np.asarray(inputs["x"], np.float32)
    pos = np.asarray(inputs["positions"], np.int32)
    w_in = np.asarray(inputs["w_in"], np.float32)[0]
    w1 = np.ascontiguousarray(np.concatenate([w_in[:, 256:384], w_in[:, 320:416], w_in[:, 320:384], w_in[:, 400:416], w_in[:, 384:400]], axis=1))
    w2 = np.ascontiguousarray(w_in[:, 0:256])
    qs_cols = []
    for u in range(4):
        qs_cols.append(w_in[:, 416 + u * 64:416 + (u + 1) * 64])
        qs_cols.append(w_in[:, 416 + (4 + u) * 64:416 + (5 + u) * 64])
    w3 = np.ascontiguousarray(np.concatenate(qs_cols + [w_in[:, 928:1056], w_in[:, 1056:1184], w_in[:, 1184:2208]], axis=1))
    wqb = np.asarray(inputs["w_q_b"], np.float32)[0]
    wqbs = wqb.copy()
    for h in range(8):
        b0 = h * 96 + 64
        wqbs[:, b0:b0 + 16] = wqb[:, b0 + 16:b0 + 32]
        wqbs[:, b0 + 16:b0 + 32] = wqb[:, b0:b0 + 16]
    wkvb = np.ascontiguousarray(np.asarray(inputs["w_kv_b"], np.float32)[0])
    wo = np.ascontiguousarray(np.asarray(inputs["w_out"], np.float32)[0])
    g1 = np.ascontiguousarray(np.asarray(inputs["norm_gain"], np.float32)[0].reshape(8, 128).T)
    gq = np.ascontiguousarray(np.asarray(inputs["q_a_norm"], np.float32)[0].reshape(2, 128).T)
    gk = np.ascontiguousarray(np.asarray(inputs["kv_a_norm"], np.float32)[0].reshape(128, 1))
    sinks = np.ascontiguousarray(np.asarray(inputs["sinks"], np.float32)[0].reshape(1, 8))
    rb = np.ascontiguousarray(np.asarray(inputs["rel_bias"], np.float32).reshape(1, 256))
    fg = np.ascontiguousarray(np.asarray(inputs["final_norm"], np.float32).reshape(1, D_MODEL))
    consts = np.zeros((128, 8), np.float32)
    pidx = np.arange(128)
    consts[:, 0] = (10000.0 ** (-((pidx % 16).astype(np.float64)) / 16.0)).astype(np.float32)
    consts[:, 1] = np.where((pidx % 32) < 16, -1.0, 1.0)
    in_maps = []
    for c in range(2 * B):
        b, p = c // 2, c % 2
        xb = x[b]
        xT = np.ascontiguousarray(xb.T)
        xTo = np.zeros((D_MODEL, NSL * 640), np.float32)
        xo = np.zeros((NSL * 512, D_MODEL), np.float32)
        po = np.zeros((1, NSL * 512), np.int32)
        valid = np.ones((128, NSL), np.float32)
        for j in range(NSL):
            ch = chunk_of(j, p)
            lo = ch * 512
            xo[j * 512:(j + 1) * 512] = xb[lo:lo + 512]
            po[0, j * 512:(j + 1) * 512] = pos[b, lo:lo + 512]
            xTo[:, j * 640 + 128:(j + 1) * 640] = xT[:, lo:lo + 512]
            if lo >= 128:
                xTo[:, j * 640:j * 640 + 128] = xT[:, lo - 128:lo]
            else:
                valid[:, j] = 0.0
        c1 = chunk_of(1, p) * 512
        pb = np.ascontiguousarray(pos[b:b + 1, c1 - 128:c1 + 128])
        in_maps.append({
            "xT_full": xT, "xT_own": xTo, "x_own": xo,
            "pos_full": np.ascontiguousarray(pos[b:b + 1]), "pos_own": po, "pos_bias": pb,
            "valid": valid, "masks": make_masks(p), "consts": consts,
            "g1": g1, "gq": gq, "gk": gk, "w1": w1, "w2": w2, "w3": w3,
            "wqb": np.ascontiguousarray(wqb), "wqbs": np.ascontiguousarray(wqbs), "wkvb": wkvb, "wo": wo,
            "sinks": sinks, "rb": rb, "fg": fg,
        })
    return in_maps


def assemble(results, B, S):
    NSL = S // 1024
    out = np.zeros((B, S, D_MODEL), np.float32)
    for c in range(2 * B):
        b, p = c // 2, c % 2
        oc = np.asarray(results[c]["out"])
        for j in range(NSL):
            ch = chunk_of(j, p)
            out[b, ch * 512:(ch + 1) * 512] = oc[j * 512:(j + 1) * 512]
    return out


def run(inputs, B, S):
    nc = build(S)
    in_maps = prep_inputs(inputs, B, S)
    res = run_bass_kernel_spmd(nc, in_maps, core_ids=list(range(2 * B)))
    return assemble(res.results, B, S)


def kernel(**inputs):
    return run(inputs, 4, 8192)
```
